# Optimizing a Trainium2 kernel written in Bass

```python
import math
import jax, jax.numpy as jnp
from jax import lax
import numpy as np

D_MODEL = 1024
BATCH = 8
SEQ = 2048
DEPTH = 1
DEC_BATCH = 128
DEC_SEQ = 4
PAST_LEN = 16384
PAGE_SIZE = 128

D_MIX = D_MODEL
D_POOL = D_MIX // 2
POOL_WINDOWS = (2, 4, 8, 16)
N_POOL_GROUPS = len(POOL_WINDOWS)
POOL_GROUP_DIM = D_POOL // N_POOL_GROUPS
POOL_BUF = max(POOL_WINDOWS) - 1
D_MLSTM = D_MIX - D_POOL
N_HEADS = 4
HEAD_DIM = D_MLSTM // N_HEADS
CHUNK = 64
D_IN = D_POOL + 4 * D_MLSTM + 2 * N_HEADS
SPLITS = (D_POOL, D_POOL + D_MLSTM, D_POOL + 2 * D_MLSTM, D_POOL + 3 * D_MLSTM,
          D_POOL + 4 * D_MLSTM, D_POOL + 4 * D_MLSTM + N_HEADS)
N_KEYS = 128
N_EXPERTS = N_KEYS * N_KEYS
PEER_HEADS = 8
D_KEY = 256
HALF_KEY = D_KEY // 2
TOPK = 16
TOKEN_BLOCK = 256
ALPHA = (2.0 * DEPTH) ** 0.25
BETA = (8.0 * DEPTH) ** -0.25
LN_EPS = 1e-5

kernel_name = 'hymba_pool_mlstm_peer_adaln_step'


def _ln(x, gain=None, bias=None):
    xf = x.astype(jnp.float32)
    mu = jnp.mean(xf, -1, keepdims=True)
    var = jnp.mean(jnp.square(xf - mu), -1, keepdims=True)
    y = (xf - mu) * lax.rsqrt(var + LN_EPS)
    if gain is not None:
        y = y * gain.astype(jnp.float32) + bias.astype(jnp.float32)
    return y.astype(x.dtype)


def _pool_mix(u, buf, pos0, w_pool, pool_scale):
    B, T, _ = u.shape
    z = jnp.concatenate([buf.astype(u.dtype), u], axis=1)
    cs = jnp.concatenate([jnp.zeros((B, 1, D_POOL), jnp.float32),
                          jnp.cumsum(z.astype(jnp.float32), axis=1)], axis=1)
    end = cs[:, POOL_BUF + 1:POOL_BUF + 1 + T]
    pos = pos0 + jnp.arange(T)
    means = []
    for g, w in enumerate(POOL_WINDOWS):
        sl = slice(g * POOL_GROUP_DIM, (g + 1) * POOL_GROUP_DIM)
        start = cs[:, POOL_BUF + 1 - w:POOL_BUF + 1 - w + T, sl]
        cnt = jnp.minimum(pos + 1, w).astype(jnp.float32)[None, :, None]
        means.append((end[:, :, sl] - start) / cnt)
    d = jnp.concatenate(means, -1) - u.astype(jnp.float32)
    d = d.reshape(B, T, N_POOL_GROUPS, POOL_GROUP_DIM).astype(u.dtype)
    y = jnp.einsum('btgc,gcd->btgd', d, w_pool).reshape(B, T, D_POOL)
    return y * pool_scale, z[:, -POOL_BUF:]


def _mlstm(q, k, v, ig, lf, S0, n0, m0):
    B, H, T, DH = q.shape
    L = CHUNK if T % CHUNK == 0 else T
    nc = T // L

    def split(a):
        return jnp.moveaxis(a.reshape(B, H, nc, L, *a.shape[3:]), 2, 0)

    causal = jnp.tril(jnp.ones((L, L), bool))

    def step(carry, inp):
        S, n, m = carry
        qc, kc, vc, ic, fc = inp
        b = jnp.cumsum(fc, axis=-1)
        a_prev = b + m[..., None]
        d = b[..., :, None] - b[..., None, :] + ic[..., None, :]
        d = jnp.where(causal, d, -jnp.inf)
        m_t = jnp.maximum(a_prev, jnp.max(d, -1))
        w_prev = jnp.exp(a_prev - m_t)
        qk = jnp.einsum('bhtd,bhsd->bhts', qc, kc) * jnp.exp(d - m_t[..., None])
        num = w_prev[..., None] * jnp.einsum('bhtd,bhde->bhte', qc, S) + jnp.einsum('bhts,bhse->bhte', qk, vc)
        den = w_prev * jnp.einsum('bhtd,bhd->bht', qc, n) + jnp.sum(qk, -1)
        h = num / jnp.maximum(jnp.abs(den), jnp.exp(-m_t))[..., None]
        m_new = m_t[..., -1]
        g_prev = jnp.exp(b[..., -1] + m - m_new)
        g_in = jnp.exp(b[..., -1:] - b + ic - m_new[..., None])
        kg = kc * g_in[..., None]
        S_new = g_prev[..., None, None] * S + jnp.einsum('bhsd,bhse->bhde', kg, vc)
        n_new = g_prev[..., None] * n + jnp.sum(kg, -2)
        return (S_new, n_new, m_new), h

    (S, n, m), hs = lax.scan(step, (S0, n0, m0), (split(q), split(k), split(v), split(ig), split(lf)))
    h = jnp.moveaxis(hs, 0, 2).reshape(B, H, T, DH)
    return h, S, n, m


def _mixer(u, pool_buf, S0, n0, m0, pos0, w_in, b_in, w_pool, pool_scale, mh_norm_g, w_out):
    B, T, _ = u.shape
    p = u @ w_in + b_in
    pu, q, k, v, o, gi, gf = jnp.split(p, SPLITS, axis=-1)
    y_pool, pool_new = _pool_mix(pu, pool_buf, pos0, w_pool, pool_scale)

    def heads(a):
        return a.reshape(B, T, N_HEADS, HEAD_DIM).transpose(0, 2, 1, 3).astype(jnp.float32)

    qh, kh, vh = heads(q), heads(k) * (HEAD_DIM ** -0.5), heads(v)
    ig = gi.astype(jnp.float32).transpose(0, 2, 1)
    lf = jax.nn.log_sigmoid(gf.astype(jnp.float32)).transpose(0, 2, 1)
    h, S, n, m = _mlstm(qh, kh, vh, ig, lf, S0.astype(jnp.float32), n0.astype(jnp.float32), m0.astype(jnp.float32))
    h = _ln(h).transpose(0, 2, 1, 3).reshape(B, T, D_MLSTM)
    y_ml = (h * mh_norm_g.astype(jnp.float32) * jax.nn.sigmoid(o.astype(jnp.float32))).astype(u.dtype)
    y = jnp.concatenate([y_pool, y_ml], -1) @ w_out
    return y, pool_new, S, n, m


def _peer(u, w_q, sub_keys, u_tab, v_tab):
    B, T, D = u.shape
    n = B * T
    blk = min(TOKEN_BLOCK, n)
    nb = -(-n // blk)
    xp = jnp.pad(u.reshape(n, D), ((0, nb * blk - n), (0, 0))).reshape(nb, blk, D)
    keys = sub_keys.astype(jnp.float32)

    def one(xb):
        q = (xb @ w_q).reshape(blk, PEER_HEADS, 2, HALF_KEY).astype(jnp.float32)
        s = jnp.einsum('nhpc,hpkc->nhpk', q, keys)
        sv, si = lax.top_k(s, TOPK)
        comb = (sv[:, :, 0, :, None] + sv[:, :, 1, None, :]).reshape(blk, PEER_HEADS, TOPK * TOPK)
        cidx = (si[:, :, 0, :, None] * N_KEYS + si[:, :, 1, None, :]).reshape(blk, PEER_HEADS, TOPK * TOPK)
        top, sel = lax.top_k(comb, TOPK)
        eidx = jnp.take_along_axis(cidx, sel, axis=-1)
        g = jax.nn.softmax(top, axis=-1)
        act = jax.nn.gelu(jnp.einsum('nd,nhkd->nhk', xb, u_tab[eidx]).astype(jnp.float32), approximate=False)
        coef = (g * act).astype(xb.dtype)
        return jnp.einsum('nhk,nhkd->nd', coef, v_tab[eidx])

    y = lax.map(one, xp).reshape(nb * blk, D)[:n]
    return y.reshape(B, T, D)


def _layer(x, c, pool_buf, S0, n0, m0, pos0, w_mod, b_mod, w_in, b_in, w_pool, pool_scale,
           mh_norm_g, w_out, ln1_g, ln1_b, w_q, sub_keys, u_tab, v_tab, ln2_g, ln2_b):
    mod = jax.nn.silu(c) @ w_mod + b_mod
    sh1, sc1, g1, sh2, sc2, g2 = [a[:, None, :] for a in jnp.split(mod, 6, axis=-1)]
    u1 = _ln(x) * (1 + sc1) + sh1
    y, pool_new, S, n, m = _mixer(u1, pool_buf, S0, n0, m0, pos0, w_in, b_in, w_pool, pool_scale, mh_norm_g, w_out)
    x = _ln(ALPHA * x + (1 + g1) * y, ln1_g, ln1_b)
    u2 = _ln(x) * (1 + sc2) + sh2
    y = _peer(u2, w_q, sub_keys, u_tab, v_tab)
    x = _ln(ALPHA * x + (1 + g2) * y, ln2_g, ln2_b)
    return x, pool_new, S, n, m


def setup_inputs(seed: int = 0) -> dict:
    key = jax.random.key(seed)
    ks = jax.random.split(key, 32)

    def nrm(k, shape, s):
        return jax.random.normal(k, shape, jnp.float32) * s

    L = DEPTH
    b_in = nrm(ks[12], (L, D_IN), 0.02)
    b_in = b_in.at[:, D_IN - N_HEADS:].add(jnp.linspace(3.0, 6.0, N_HEADS))
    return {
        'x_prompt': nrm(ks[0], (BATCH, SEQ, D_MODEL), 1.0),
        'x_sample': nrm(ks[1], (DEC_BATCH, DEC_SEQ, D_MODEL), 1.0),
        'c_prompt': nrm(ks[2], (BATCH, D_MODEL), 1.0),
        'c_sample': nrm(ks[3], (DEC_BATCH, D_MODEL), 1.0),
        'state_pool': nrm(ks[4], (L, DEC_BATCH, POOL_BUF, D_POOL), 1.0),
        'state_C': nrm(ks[5], (L, DEC_BATCH, N_HEADS, HEAD_DIM, HEAD_DIM), 0.1),
        'state_n': nrm(ks[6], (L, DEC_BATCH, N_HEADS, HEAD_DIM), 0.3),
        'state_m': nrm(ks[7], (L, DEC_BATCH, N_HEADS), 1.0),
        'w_mod': nrm(ks[8], (L, D_MODEL, 6 * D_MODEL), 0.1 * D_MODEL ** -0.5),
        'b_mod': nrm(ks[9], (L, 6 * D_MODEL), 0.02),
        'w_in': nrm(ks[10], (L, D_MODEL, D_IN), D_MODEL ** -0.5),
        'b_in': b_in,
        'w_pool': nrm(ks[13], (L, N_POOL_GROUPS, POOL_GROUP_DIM, POOL_GROUP_DIM), POOL_GROUP_DIM ** -0.5),
        'pool_scale': 1.0 + nrm(ks[14], (L, D_POOL), 0.1),
        'mh_norm_g': 1.0 + nrm(ks[15], (L, D_MLSTM), 0.1),
        'w_out': nrm(ks[16], (L, D_MIX, D_MODEL), BETA * D_MIX ** -0.5),
        'ln1_g': 1.0 + nrm(ks[17], (L, D_MODEL), 0.1),
        'ln1_b': nrm(ks[18], (L, D_MODEL), 0.02),
        'w_q': nrm(ks[19], (L, D_MODEL, PEER_HEADS * D_KEY), D_MODEL ** -0.5),
        'sub_keys': nrm(ks[20], (L, PEER_HEADS, 2, N_KEYS, HALF_KEY), HALF_KEY ** -0.5),
        'u_tab': nrm(ks[21], (L, N_EXPERTS, D_MODEL), D_MODEL ** -0.5),
        'v_tab': nrm(ks[22], (L, N_EXPERTS, D_MODEL), BETA * PEER_HEADS ** -0.5),
        'ln2_g': 1.0 + nrm(ks[23], (L, D_MODEL), 0.1),
        'ln2_b': nrm(ks[24], (L, D_MODEL), 0.02),
    }


def reference(x_prompt, x_sample, c_prompt, c_sample, state_pool, state_C, state_n, state_m,
              w_mod, b_mod, w_in, b_in, w_pool, pool_scale, mh_norm_g, w_out, ln1_g, ln1_b,
              w_q, sub_keys, u_tab, v_tab, ln2_g, ln2_b):
    B = x_prompt.shape[0]
    yp, ys = x_prompt, x_sample
    pp, Cp, np_, mp, ps, Cs, ns, ms = [], [], [], [], [], [], [], []
    for l in range(DEPTH):
        wl = (w_mod[l], b_mod[l], w_in[l], b_in[l], w_pool[l], pool_scale[l], mh_norm_g[l], w_out[l],
              ln1_g[l], ln1_b[l], w_q[l], sub_keys[l], u_tab[l], v_tab[l], ln2_g[l], ln2_b[l])
        yp, a, b, c, d = _layer(yp, c_prompt,
                                jnp.zeros((B, POOL_BUF, D_POOL), x_prompt.dtype),
                                jnp.zeros((B, N_HEADS, HEAD_DIM, HEAD_DIM), jnp.float32),
                                jnp.zeros((B, N_HEADS, HEAD_DIM), jnp.float32),
                                jnp.zeros((B, N_HEADS), jnp.float32), 0, *wl)
        pp.append(a); Cp.append(b); np_.append(c); mp.append(d)
        ys, a, b, c, d = _layer(ys, c_sample, state_pool[l], state_C[l], state_n[l], state_m[l], PAST_LEN, *wl)
        ps.append(a); Cs.append(b); ns.append(c); ms.append(d)
    sd = state_C.dtype
    pool_p = jnp.stack(pp).astype(state_pool.dtype)
    C_p = jnp.stack(Cp).astype(sd)
    n_p = jnp.stack(np_).astype(state_n.dtype)
    m_p = jnp.stack(mp).astype(state_m.dtype)
    pool_s = jnp.stack(ps).astype(state_pool.dtype)
    C_s = jnp.stack(Cs).astype(sd)
    n_s = jnp.stack(ns).astype(state_n.dtype)
    m_s = jnp.stack(ms).astype(state_m.dtype)
    return (yp, ys, pool_p, C_p, n_p, m_p, pool_s, C_s, n_s, m_s)
```

```python
import numpy as np
from concourse.bass_utils import run_bass_kernel_spmd
import concourse.bass as bass
import concourse.mybir as mybir
from contextlib import ExitStack

F32 = mybir.dt.float32
BF16 = mybir.dt.bfloat16
U32 = mybir.dt.uint32
I32 = mybir.dt.int32
AF = mybir.ActivationFunctionType
ALU = mybir.AluOpType
AX = mybir.AxisListType

ENGS = ("pe", "dve", "act", "pool", "sp")
STRICT_SYNC = True


class Buf:
    def __init__(self, prog, t, name):
        self.p = prog
        self.t = t
        self.name = name
        self.lw = None
        self.rd = {}
        self.dsem = None
        self.dcnt = 0

    def __getitem__(self, k):
        return self.t[k]

    def ap(self):
        return self.t[:]


class Prog:
    def __init__(self, nc):
        self.nc = nc
        self.es = ExitStack()
        self.streams = {e: [] for e in ENGS}
        self.sem = {}
        for e in ENGS:
            self.sem[e] = nc.alloc_semaphore(name=f"sem_{e}")
        self.cnt = {e: 0 for e in ENGS}
        self.known = {e: {} for e in ENGS}
        self.out_waits = []
        self.nbuf = 0
        self.cur = self.es
        self.dbufs = []

    def sbuf(self, shape, dt, name=None):
        self.nbuf += 1
        name = name or f"sb{self.nbuf}"
        t = self.cur.enter_context(self.nc.sbuf_tensor(name, list(shape), dt))
        return Buf(self, t, name)

    def psum(self, shape, dt, name=None):
        self.nbuf += 1
        name = name or f"ps{self.nbuf}"
        t = self.es.enter_context(self.nc.psum_tensor(name, list(shape), dt))
        return Buf(self, t, name)

    def dram(self, name, shape, dt, kind="Internal"):
        t = self.nc.dram_tensor(name, list(shape), dt, kind=kind).ap()
        return Buf(self, t, name)

    def _deps(self, eng, reads, writes, is_dma=False):
        deps = {}

        def add(d, same_ok):
            if d is None:
                return
            kind, k, v = d
            if kind == "e" and k == eng and same_ok and not is_dma and (not STRICT_SYNC or eng == "pe"):
                return
            key = (kind, k)
            if deps.get(key, 0) < v:
                deps[key] = v

        for r in reads:
            add(r.lw, same_ok=(eng == "pe" or eng == "sp"))
        for w in writes:
            add(w.lw, same_ok=True)
            for d in w.rd.values():
                add(d, same_ok=True)
        waits = []
        kn = self.known[eng]
        for key, v in deps.items():
            if kn.get(key, 0) < v:
                kn[key] = v
                s = self.sem[key[1]] if key[0] == "e" else key[1]
                waits.append((s, v))
        return waits

    def op(self, eng, fn, reads=(), writes=()):
        waits = self._deps(eng, reads, writes)
        self.cnt[eng] += 1
        c = self.cnt[eng]
        sem = self.sem[eng]

        def emit(e, waits=waits, fn=fn, sem=sem):
            for s, v in waits:
                e.wait_ge(s, v)
            ins = fn(e)
            ins.then_inc(sem, 1)

        self.streams[eng].append(emit)
        tag = ("e", eng, c)
        for w in writes:
            w.lw = tag
            w.rd = {}
        for r in reads:
            if r not in writes:
                r.rd[("e", eng)] = tag
        return tag

    def dma(self, eng, fn, reads=(), writes=(), sem_buf=None, is_output=False):
        waits = self._deps(eng, reads, writes, is_dma=True)
        sb = sem_buf or (writes[0] if writes else reads[0])
        if sb.dsem is None:
            sb.dsem = self.nc.alloc_semaphore(name=f"dsem_{sb.name}")
            self.dbufs.append(sb)
        sb.dcnt += 16
        val = sb.dcnt
        dsem = sb.dsem

        def emit(e, waits=waits, fn=fn, dsem=dsem):
            for s, v in waits:
                e.wait_ge(s, v)
            ins = fn(e)
            ins.then_inc(dsem, 16)

        self.streams[eng].append(emit)
        tag = ("d", dsem, val)
        for w in writes:
            w.lw = tag
            w.rd = {}
        for r in reads:
            if r not in writes:
                r.rd[("d", dsem)] = tag
        if is_output:
            self.out_waits.append((dsem, val))
        return tag

    def begin_phase(self):
        self.cur = ExitStack()

    def end_phase(self):
        cnts = dict(self.cnt)
        dm = [(b.dsem, b.dcnt) for b in self.dbufs]
        for en in ENGS:
            def emit_bar(e, en=en, cnts=cnts, dm=dm):
                for e2 in ENGS:
                    if e2 != en and cnts[e2] > 0:
                        e.wait_ge(self.sem[e2], cnts[e2])
                for s_, v_ in dm:
                    e.wait_ge(s_, v_)
            self.streams[en].append(emit_bar)
            for e2 in ENGS:
                self.known[en][("e", e2)] = max(self.known[en].get(("e", e2), 0), cnts[e2])
            for s_, v_ in dm:
                self.known[en][("d", s_)] = max(self.known[en].get(("d", s_), 0), v_)
        self._emit_block()
        self.streams = {e: [] for e in ENGS}
        self.cur.close()
        self.cur = self.es

    def _emit_block(self):
        nc = self.nc
        with nc.Block() as block:
            @block.sync
            def _(e):
                for f in self.streams["sp"]:
                    f(e)

            @block.tensor
            def _(e):
                for f in self.streams["pe"]:
                    f(e)

            @block.vector
            def _(e):
                for f in self.streams["dve"]:
                    f(e)

            @block.scalar
            def _(e):
                for f in self.streams["act"]:
                    f(e)

            @block.gpsimd
            def _(e):
                for f in self.streams["pool"]:
                    f(e)

    def finish(self):
        nc = self.nc
        finals = {}
        for s, v in self.out_waits:
            finals[s] = max(finals.get(s, 0), v)
        fin = list(finals.items())

        def emit_final(e, fin=fin):
            for s, v in fin:
                e.wait_ge(s, v)

        self.streams["sp"].append(emit_final)
        self._emit_block()
        self.es.close()

NTP = 16
DM = 1024
ALPHA_ = 2.0 ** 0.25
EPS_ = 1e-5
KS_ = 128.0 ** -0.5
NSLOT = 128
DEBUG = False


def build_program(ntiles_prompt=NTP, do_sample=True, do_peer=True):
    nc = bass.Bass("TRN2", target_bir_lowering=False)

    def din(name, shape, dt=F32):
        return nc.dram_tensor(name, list(shape), dt, kind="ExternalInput").ap()

    def dout(name, shape, dt=F32):
        return nc.dram_tensor(name, list(shape), dt, kind="ExternalOutput").ap()

    xp = din("xp", [2048, DM]); xs = din("xs", [128, DM]); cc = din("cc", [17, DM])
    w_mod = din("w_mod", [DM, 6144]); b_mod = din("b_mod", [1, 6144])
    w_in = din("w_in", [DM, 2568]); b_fm = din("b_fm", [128, 20]); b_gate = din("b_gate", [4, 2])
    b_row = din("b_row", [1, 2568]); w_pool = din("w_pool", [4, 128, 128])
    pscale = din("pscale", [128, 4]); mhg = din("mhg", [128, 4]); w_out = din("w_out", [DM, DM])
    ln1g = din("ln1g", [1, DM]); ln1b = din("ln1b", [1, DM]); ln2g = din("ln2g", [1, DM]); ln2b = din("ln2b", [1, DM])
    w_q = din("w_q", [16, 128, 8, 128]); keysT = din("keysT", [128, 16, 128])
    u_tab = din("u_tab", [16384, DM]); v_tab = din("v_tab", [16384, DM])
    spool = din("spool", [128, 4, 16, 15]); spool_raw = din("spool_raw", [16, 15, 512]); S0d = din("S0", [16, 4, 128, 128]); n0fm = din("n0fm", [128, 4, 16])
    m0T = din("m0T", [4, 32]); m0col = din("m0col", [128, 4])
    c_ident = din("c_ident", [128, 128]); c_maskp = din("c_maskp", [128, 128]); c_masks = din("c_masks", [128, 128])
    c_bmask = din("c_bmask", [128, 16]); c_selh = din("c_selh", [4, 512]); c_iota = din("c_iota", [128, 16])
    c_rc0 = din("c_rc0", [128, 4, 128]); c_rcw = din("c_rcw", [128, 4, 128])

    yp = dout("yp", [2048, DM]); ys = dout("ys", [64, DM]); pool_p = dout("pool_p", [15, 512])
    C_p = dout("C_p", [4, 128, 128]); n_p = dout("n_p", [4, 128]); m_p = dout("m_p", [4, 1])
    pool_s = dout("pool_s", [16, 15, 512]); C_s = dout("C_s", [16, 4, 128, 128]); n_s = dout("n_s", [16, 4, 128])
    m_s = dout("m_s", [4, 16])

    P = Prog(nc)
    mod_dram = P.dram("mod_scr", [17, 6144], F32)
    x1_dram = P.dram("x1_scr", [17 * 128, DM], F32)
    TAB = P.dram("tab_scr", [16384, 2048], BF16)
    P.begin_phase()
    sb = P.sbuf

    defer = [None]

    def T(eng, fn, r, w):
        if defer[0] is not None:
            defer[0].append(lambda: P.op(eng, fn, reads=r, writes=w))
            return None
        return P.op(eng, fn, reads=r, writes=w)

    def LD(eng, out_ap, in_ap, w, r=()):
        if defer[0] is not None:
            defer[0].append(lambda: P.dma(eng, lambda e: e.dma_start(out=out_ap, in_=in_ap), reads=list(r), writes=[w]))
            return None
        return P.dma(eng, lambda e: e.dma_start(out=out_ap, in_=in_ap), reads=list(r), writes=[w])

    def ST(eng, out_ap, in_ap, r, final=True):
        return P.dma(eng, lambda e: e.dma_start(out=out_ap, in_=in_ap), reads=[r], writes=[], sem_buf=r, is_output=final)

    PSB = [P.psum([128, 512], F32, f"psb{i}") for i in range(8)]
    rot = [0]

    def ps():
        b = PSB[rot[0] % 6]
        rot[0] += 1
        return b
    PSY = [PSB[6], PSB[7]]

    ident = sb([128, 128], F32, "ident"); identb = sb([128, 128], BF16, "identb")
    maskp = sb([128, 128], F32, "maskp"); masks = sb([128, 128], F32, "masks")
    bmask = sb([128, 16], F32, "bmask"); selh = sb([4, 512], F32, "selh"); iota16 = sb([128, 16], F32, "iota16")
    rc0 = sb([128, 4, 128], F32, "rc0"); rcw = sb([128, 4, 128], F32, "rcw")
    LD("sp", ident[:], c_ident, ident); LD("sp", maskp[:], c_maskp, maskp); LD("sp", masks[:], c_masks, masks)
    LD("sp", bmask[:], c_bmask, bmask); LD("sp", selh[:], c_selh, selh); LD("sp", iota16[:], c_iota, iota16)
    LD("sp", rc0[:], c_rc0, rc0); LD("sp", rcw[:], c_rcw, rcw)
    T("act", lambda e: e.activation(out=identb[:], in_=ident[:], func=AF.Copy), [ident], [identb])
    epsc = sb([128, 1], F32, "epsc"); onec = sb([128, 1], F32, "onec")
    T("dve", lambda e: e.memset(epsc[:], EPS_), [], [epsc])
    T("dve", lambda e: e.memset(onec[:], 1.0), [], [onec])
    ones4 = sb([4, 128], F32, "ones4")
    T("dve", lambda e: e.memset(ones4[:], 1.0), [], [ones4])
    ones_bf = sb([1, 128], BF16, "ones_bf")
    T("dve", lambda e: e.memset(ones_bf[:], 1.0), [], [ones_bf])

    bfm = sb([128, 20], F32, "bfm"); bgate = sb([4, 2], F32, "bgate"); psc = sb([128, 4], F32, "psc"); mhgs = sb([128, 4], F32, "mhgs")
    LD("sp", bfm[:], b_fm, bfm); LD("sp", bgate[:], b_gate, bgate); LD("sp", psc[:], pscale, psc); LD("sp", mhgs[:], mhg, mhgs)
    bks = sb([128, 4], F32, "bks"); nbg = sb([4, 1], F32, "nbg")
    T("dve", lambda e: e.tensor_scalar(out=bks[:], in0=bfm[:, 8:12], scalar1=KS_, scalar2=None, op0=ALU.mult), [bfm], [bks])
    T("dve", lambda e: e.tensor_scalar(out=nbg[:], in0=bgate[:, 1:2], scalar1=-1.0, scalar2=None, op0=ALU.mult), [bgate], [nbg])
    LNG = sb([128, 2, DM], F32, "LNG")

    Xp = [sb([128, DM], F32, f"Xin{i}") for i in range(2)]; X = Xp[0]; T1 = sb([128, DM], F32, "T1"); T2 = sb([128, DM], F32, "T2"); X1 = sb([128, DM], F32, "X1")
    U1b = sb([128, DM], BF16, "U1b"); U1T = sb([128, 8, 128], BF16, "U1T")
    U1b_f = sb([128, 256], F32, "U1b_f"); OUTA = sb([128, 256], F32, "OUTA")
    stats = sb([128, 2, 6], F32, "stats"); mv = sb([128, 2], F32, "mv"); lnv = sb([128, 1], F32, "lnv"); rstd = sb([128, 1], F32, "rstd")
    PU = sb([128, 4, 143], F32, "PU"); Wa = sb([128, 4, 143], F32, "Wa"); Wb = sb([128, 4, 143], F32, "Wb")
    PUS = sb([128, 4, 16, 19], F32, "PUS"); WaS = sb([128, 4, 16, 19], F32, "WaS"); WbS = sb([128, 4, 16, 19], F32, "WbS")
    Mm = sb([128, 4, 128], F32, "Mm"); dT = sb([128, 4, 128], BF16, "dT")
    qT = sb([128, 4, 128], BF16, "qT"); kT = sb([128, 4, 128], BF16, "kT"); sgT = sb([128, 4, 128], BF16, "sgT")
    ktok = sb([128, 512], BF16, "ktok"); vtok = sb([128, 4, 129], BF16, "vtok"); PUT = sb([128, 512], F32, "PUT")
    ymixT = sb([128, 8, 128], BF16, "ymixT")
    IG = sb([4, 128], F32, "IG"); E1 = sb([4, 128], F32, "E1"); L1 = sb([4, 128], F32, "L1")
    Bt = sb([4, 128], F32, "Bt"); Gt = sb([4, 128], F32, "Gt"); Mgt = sb([4, 128], F32, "Mgt"); NMt = sb([4, 128], F32, "NMt"); MgE = sb([4, 128], F32, "MgE")
    Bprev = sb([4, 1], F32, "Bprev"); Mprev = sb([4, 1], F32, "Mprev")
    COLS = sb([128, 16], F32, "COLS"); MgB = sb([128, 4, 128], F32, "MgB"); MgprevB = sb([128, 4], F32, "MgprevB")
    tmp4 = sb([128, 4], F32, "tmp4"); tmp4b = sb([128, 4], F32, "tmp4b"); tmp4c = sb([128, 4], F32, "tmp4c"); WP = sb([128, 4], F32, "WP"); EMT = sb([128, 4], F32, "EMT"); GIN = sb([128, 4], F32, "GIN"); GP = sb([128, 4], F32, "GP")
    Aw = [sb([128, 128], F32, f"Aw{i}") for i in range(2)]; Ew = [sb([128, 128], F32, f"Ew{i}") for i in range(2)]
    PT = [sb([128, 128], BF16, f"PT{i}") for i in range(2)]
    TMPN = [sb([128, 129], F32, f"TMPN{i}") for i in range(2)]; ND = sb([128, 4, 129], F32, "ND")
    absd = sb([128, 4], F32, "absd"); RD = sb([128, 4], F32, "RD")
    HH = sb([128, 4, 128], F32, "HH"); st4 = sb([128, 4, 6], F32, "st4"); MV4 = sb([128, 4, 2], F32, "MV4"); LN4 = sb([128, 4], F32, "LN4"); RS4 = sb([128, 4], F32, "RS4")
    HN = sb([128, 4, 128], BF16, "HN")
    KG = [sb([128, 128], BF16, f"KG{i}") for i in range(2)]
    S32 = sb([128, 4, 129], F32, "S32"); Sbf = sb([128, 4, 129], BF16, "Sbf")
    for tz in (S32, Sbf, MgprevB, Bprev, Mprev, PU, dT, Mm):
        T("dve", lambda e, tz=tz: e.memset(tz[:], 0.0), [], [tz])
    T("pool", lambda e: e.memset(vtok[:, :, 128:129], 1.0), [], [vtok])

    if True:
        m0T_sb = sb([4, 32], F32, "m0T_sb"); m0col_sb = sb([128, 4], F32, "m0col_sb"); m0B = sb([128, 4, 32], F32, "m0B")
        LD("sp", m0T_sb[:], m0T, m0T_sb); LD("sp", m0col_sb[:], m0col, m0col_sb)
        qTz = sb([128, 16 * 132], BF16, "qTz")
        T("pool", lambda e: e.memset(qTz[:], 0.0), [], [qTz])
        S0h = sb([128, 16, 129], F32, "S0h"); S0h2 = sb([128, 16, 129], F32, "S0h2"); n0all = sb([128, 4, 16], F32, "n0all"); S0bf = sb([128, 16, 129], BF16, "S0bf"); SN = sb([128, 16, 129], F32, "SN")
        GINM = sb([128, 16], F32, "GINM"); KGZ = sb([128, 16, 128], BF16, "KGZ"); GPB = sb([128, 4, 16], F32, "GPB")
        NS = sb([16, 128], F32, "NS"); MSo = sb([4, 16], F32, "MSo")

    w_in_bf = sb([128, 8, 2568], BF16, "w_in_bf"); w_out_bf = sb([128, 8, DM], BF16, "w_out_bf")
    w_pool_bf = sb([128, 4, 128], BF16, "w_pool_bf"); brow_bf = sb([1, 2568], BF16, "brow_bf")
    stgb = [S0h, SN]
    S0hp = [S0h, S0h2]
    cast_eng = ["dve", "act"]
    nci = [0]

    def cast(out_ap, in_ap, r, w):
        en = cast_eng[nci[0] % 2]
        nci[0] += 1
        if en == "act":
            T("act", lambda e: e.activation(out=out_ap, in_=in_ap, func=AF.Copy), r, w)
        else:
            T(en, lambda e: e.tensor_copy(out_ap, in_ap), r, w)
    def fl(b_):
        t_ = b_[:]
        return t_ if len(t_.shape) == 2 else t_.rearrange("p a b -> p (a b)")
    for k in range(8):
        P.dma("pool", lambda e, k=k: e.dma_start(out=w_in_bf[:, k, :], in_=w_in[k * 128:(k + 1) * 128, :]), reads=[], writes=[], sem_buf=w_in_bf)
    w_in_bf.lw = ("d", w_in_bf.dsem, w_in_bf.dcnt)
    for k in range(8):
        P.dma("pool", lambda e, k=k: e.dma_start(out=w_out_bf[:, k, :], in_=w_out[k * 128:(k + 1) * 128, :]), reads=[], writes=[], sem_buf=w_out_bf)
    w_out_bf.lw = ("d", w_out_bf.dsem, w_out_bf.dcnt)
    P.dma("pool", lambda e: e.dma_start(out=w_pool_bf[:], in_=w_pool.rearrange("g c d -> c g d")), reads=[], writes=[w_pool_bf])
    P.dma("pool", lambda e: e.dma_start(out=brow_bf[:], in_=b_row), reads=[], writes=[brow_bf])

    cc_sb = X1; scT = T1
    LD("sp", cc_sb[0:17, :], cc, cc_sb)
    pb = ps()
    def _tr(e, pb=pb):
        for k in range(8):
            ins = e.transpose(out=pb[:, k * 17:(k + 1) * 17], in_=cc_sb[0:17, k * 128:(k + 1) * 128], identity=ident[0:17, 0:17])
        return ins
    T("pe", _tr, [cc_sb, ident], [pb])
    T("act", lambda e, pb=pb: e.activation(out=scT[:, 0:136], in_=pb[:, 0:136], func=AF.Silu), [pb], [scT])
    w_mod_v = w_mod.rearrange("(k p) c -> p k c", p=128)
    MB = sb([128, 3072], F32, "MB")
    stg4 = [S0h, SN, MB, LNG]
    bm3 = [T2, U1b_f, X1]; mc3 = [Xp[0], OUTA, Xp[1]]
    for c in range(24):
        wmb = stg4[c % 4]; wm = fl(wmb)[:, 0:2048].rearrange("p (k c) -> p k c", k=8)
        bmb = bm3[c % 3]; mcb = mc3[c % 3]
        bm = bmb[0:17, 0:256]; mc = mcb[0:17, 0:256]
        LD("sp", wm, w_mod_v[:, :, c * 256:(c + 1) * 256], wmb)
        LD("sp", bm, b_mod[0:1, c * 256:(c + 1) * 256].broadcast_to([17, 256]), bmb)
        pb = ps()
        def _mm(e, pb=pb, wm=wm):
            for k in range(8):
                ins = e.matmul(pb[0:17, 0:256], lhsT=scT[:, k * 17:(k + 1) * 17], rhs=wm[:, k, :], start=(k == 0), stop=(k == 7))
            return ins
        T("pe", _mm, [scT, wmb], [pb])
        plus1 = 1.0 if (c // 4) in (1, 2, 4, 5) else 0.0
        T("dve", lambda e, pb=pb, bm=bm, mc=mc, plus1=plus1: e.scalar_tensor_tensor(out=mc, in0=pb[0:17, 0:256], scalar=plus1, in1=bm, op0=ALU.add, op1=ALU.add), [pb, bmb], [mcb])
        P.dma("act", lambda e, mc=mc, c=c: e.dma_start(out=mod_dram[:, c * 256:(c + 1) * 256], in_=mc), reads=[mcb], writes=[mod_dram], sem_buf=mod_dram)
    LD("sp", MB[:], mod_dram[0:1, 0:3072].broadcast_to([128, 3072]), MB, r=[mod_dram])
    for j, a in enumerate((ln1g, ln1b)):
        LD("act", LNG[:, j, :], a.broadcast_to([128, DM]), LNG)
    NCV = 16
    RCV = 16384 // NCV
    for c in range(NCV):
        for ti_, src in enumerate((u_tab, v_tab)):
            P.dma("pool", lambda e, c=c, ti_=ti_, src=src: e.dma_start(out=TAB[c * RCV:(c + 1) * RCV, ti_ * 1024:(ti_ + 1) * 1024], in_=src[c * RCV:(c + 1) * RCV, :]), reads=[MB], writes=[], sem_buf=TAB)
    TAB.lw = ("d", TAB.dsem, TAB.dcnt)


    LNSET = [(stats, mv, lnv, rstd)]
    dbg = {}
    if DEBUG:
        dbg["mb"] = dout("dbg_mb", [128, 3072]); dbg["u1b"] = dout("dbg_u1b", [128, DM], BF16); dbg["pu"] = dout("dbg_pu", [128, 4, 143])
        dbg["qT"] = dout("dbg_qT", [128, 4, 128], BF16); dbg["ymixT"] = dout("dbg_ymixT", [128, 8, 128], BF16); dbg["nd"] = dout("dbg_nd", [128, 4, 129])
        dbg["cols"] = dout("dbg_cols", [128, 16]); dbg["u1T"] = dout("dbg_u1T", [128, 8, 128], BF16)
        ST("sp", dbg["mb"], MB[:], MB)

    def ln_stats(src, sset=None):
        if sset is not None:
            stats, mv, lnv, rstd = sset
        else:
            stats, mv, lnv, rstd = LNSET[0]
        for j in range(2):
            T("dve", lambda e, j=j: e.bn_stats(out=stats[:, j, :], in_=src[:, j * 512:(j + 1) * 512]), [src], [stats])
        T("dve", lambda e: e.bn_aggr(out=mv[:], in_=stats[:].rearrange("p a b -> p (a b)")), [stats], [mv])
        T("act", lambda e: e.activation(out=lnv[:], in_=mv[:, 1:2], func=AF.Ln, bias=epsc[:], scale=1.0), [mv, epsc], [lnv])
        T("act", lambda e: e.activation(out=rstd[:], in_=lnv[:], func=AF.Exp, scale=-0.5), [lnv], [rstd])

    def ln_apply(src, dst, sset=None):
        if sset is not None:
            stats, mv, lnv, rstd = sset
        else:
            stats, mv, lnv, rstd = LNSET[0]
        T("dve", lambda e: e.tensor_scalar(out=dst[:], in0=src[:], scalar1=mv[:, 0:1], scalar2=rstd[:], op0=ALU.subtract, op1=ALU.mult), [src, mv, rstd], [dst])

    def ln_affine(src, tmp, dst, gbuf, g_ap, b_ap, sset=None):
        if sset is not None:
            stats, mv, lnv, rstd = sset
        else:
            stats, mv, lnv, rstd = LNSET[0]
        T("dve", lambda e: e.scalar_tensor_tensor(out=tmp[:], in0=src[:], scalar=mv[:, 0:1], in1=g_ap, op0=ALU.subtract, op1=ALU.mult), [src, mv, gbuf], [tmp])
        T("dve", lambda e: e.scalar_tensor_tensor(out=dst[:], in0=tmp[:], scalar=rstd[:], in1=b_ap, op0=ALU.mult, op1=ALU.add), [tmp, rstd, gbuf], [dst])

    def tile(i, pr):
        last_p = pr and (i == ntiles_prompt - 1)
        ti = i if pr else ntiles_prompt
        X = Xp[ti % 2]
        if ti == 0:
            LD("sp", X[:], xp[0:128, :], X)
        if pr and i + 1 < ntiles_prompt:
            LD("sp", Xp[(ti + 1) % 2][:], xp[(i + 1) * 128:(i + 2) * 128, :], Xp[(ti + 1) % 2])
        elif pr and do_sample:
            LD("sp", Xp[(ti + 1) % 2][:], xs, Xp[(ti + 1) % 2])
        ln_stats(X); ln_affine(X, T1, U1b, MB, MB[:, 1024:2048], MB[:, 0:1024])
        pb = ps(); pbv = pb.t[:].bitcast(BF16)
        def _tr(e, pbv=pbv):
            for k in range(8):
                ins = e.transpose(out=pbv[:, k * 128:(k + 1) * 128], in_=U1b[:, k * 128:(k + 1) * 128], identity=identb[:])
            return ins
        T("pe", _tr, [U1b, identb], [pb])
        T("act", lambda e, pbv=pbv: e.activation(out=U1T[:].rearrange("p k t -> p (k t)"), in_=pbv[:, 0:1024], func=AF.Copy), [pb], [U1T])
        if DEBUG and pr and i == 0:
            ST("sp", dbg["u1b"], U1b[:], U1b); ST("sp", dbg["u1T"], U1T[:], U1T)
        def fm_group(c0):
            pb = ps()
            def _mm(e, pb=pb):
                for j in range(4):
                    for k in range(8):
                        ins = e.matmul(pb[:, j * 128:(j + 1) * 128], lhsT=w_in_bf[:, k, c0 + j * 128:c0 + (j + 1) * 128], rhs=U1T[:, k, :], start=(k == 0), stop=(k == 7))
                return ins
            T("pe", _mm, [w_in_bf, U1T], [pb])
            return pb
        pb = fm_group(0)
        for j in range(4):
            if pr:
                o = PU[:, j, 15:143]; ii = pb[:, j * 128:(j + 1) * 128]; wbuf = PU
            else:
                o = PUS[:, j, :, 15:19]; ii = pb[:, j * 128:j * 128 + 64].rearrange("p (a b) -> p a b", b=4); wbuf = PUS
            T("act", lambda e, o=o, ii=ii, j=j: e.activation(out=o, in_=ii, func=AF.Identity, bias=bfm[:, j:j + 1], scale=1.0), [pb, bfm], [wbuf])
        pb = fm_group(512)
        for j in range(4):
            T("act", lambda e, pb=pb, j=j: e.activation(out=qT[:, j, :], in_=pb[:, j * 128:(j + 1) * 128], func=AF.Identity, bias=bfm[:, 4 + j:5 + j], scale=1.0), [pb, bfm], [qT])
        pb = fm_group(1024)
        for j in range(4):
            T("act", lambda e, pb=pb, j=j: e.activation(out=kT[:, j, :], in_=pb[:, j * 128:(j + 1) * 128], func=AF.Identity, bias=bks[:, j:j + 1], scale=KS_), [pb, bks], [kT])
        pb = fm_group(2048)
        for j in range(4):
            T("act", lambda e, pb=pb, j=j: e.activation(out=sgT[:, j, :], in_=pb[:, j * 128:(j + 1) * 128], func=AF.Sigmoid, bias=bfm[:, 16 + j:17 + j], scale=1.0), [pb, bfm], [sgT])
        pb = ps()
        def _mg(e, pb=pb):
            for g in range(2):
                for k in range(8):
                    ins = e.matmul(pb[0:4, g * 128:(g + 1) * 128], lhsT=w_in_bf[:, k, 2560 + 4 * g:2564 + 4 * g], rhs=U1T[:, k, :], start=(k == 0), stop=(k == 7))
            return ins
        T("pe", _mg, [w_in_bf, U1T], [pb])
        T("act", lambda e, pb=pb: e.activation(out=IG[:], in_=pb[0:4, 0:128], func=AF.Identity, bias=bgate[:, 0:1], scale=1.0), [pb, bgate], [IG])
        T("act", lambda e, pb=pb: e.activation(out=E1[:], in_=pb[0:4, 128:256], func=AF.Exp, bias=nbg[:], scale=-1.0), [pb, nbg], [E1])
        T("act", lambda e: e.activation(out=L1[:], in_=E1[:], func=AF.Ln, bias=onec[0:4, :], scale=1.0), [E1, onec], [L1])
        def tm_group(c0):
            pb = ps()
            def _mm(e, pb=pb):
                for k in range(8):
                    e.matmul(pb[:, 0:512], lhsT=U1T[:, k, :], rhs=w_in_bf[:, k, c0:c0 + 512], start=(k == 0), stop=False)
                return e.matmul(pb[:, 0:512], lhsT=ones_bf[0:1, :], rhs=brow_bf[0:1, c0:c0 + 512], start=False, stop=True)
            T("pe", _mm, [w_in_bf, U1T, ones_bf, brow_bf], [pb])
            return pb
        pb = tm_group(1024)
        T("act", lambda e, pb=pb: e.activation(out=ktok[:], in_=pb[:, 0:512], func=AF.Copy, scale=KS_), [pb], [ktok])
        pb = tm_group(1536)
        T("dve", lambda e, pb=pb: e.tensor_copy(vtok[:, :, 0:128], pb[:, 0:512].rearrange("p (h e) -> p h e", h=4)), [pb], [vtok])
        if last_p or not pr:
            pb = tm_group(0)
            T("act", lambda e, pb=pb: e.activation(out=PUT[:], in_=pb[:, 0:512], func=AF.Copy), [pb], [PUT])
            if pr:
                ST("sp", pool_p, PUT[113:128, :], PUT)
            else:
                for i4 in range(4):
                    ST("sp", pool_s[:, 11 + i4, :], PUT[i4:64:4, :], PUT)
        if pr:
            z = PU; A_, B_ = Wa, Wb
            def sl(buf, g0, g1, a, b):
                return buf[:, g0:g1, a:b]
            n0 = 15; n1 = 143
        else:
            z = PUS; A_, B_ = WaS, WbS
            def sl(buf, g0, g1, a, b):
                return buf[:, g0:g1, :, a:b]
            n0 = 15; n1 = 19
        T("dve", lambda e: e.tensor_tensor(out=sl(A_, 0, 4, 1, n1), in0=sl(z, 0, 4, 1, n1), in1=sl(z, 0, 4, 0, n1 - 1), op=ALU.add), [z], [A_])
        T("dve", lambda e: e.tensor_tensor(out=sl(B_, 1, 4, 3, n1), in0=sl(A_, 1, 4, 3, n1), in1=sl(A_, 1, 4, 1, n1 - 2), op=ALU.add), [A_], [B_])
        T("dve", lambda e: e.tensor_tensor(out=sl(A_, 2, 4, 7, n1), in0=sl(B_, 2, 4, 7, n1), in1=sl(B_, 2, 4, 3, n1 - 4), op=ALU.add), [B_], [A_])
        T("dve", lambda e: e.tensor_tensor(out=sl(B_, 3, 4, 15, n1), in0=sl(A_, 3, 4, 15, n1), in1=sl(A_, 3, 4, 7, n1 - 8), op=ALU.add), [A_], [B_])
        rc = rc0 if (pr and i == 0) else rcw
        srcs = [A_, B_, A_, B_]
        for g in range(4):
            if pr:
                o = Mm[:, g:g + 1, :]; r_ = rc[:, g:g + 1, :]
            else:
                o = Mm[:, g:g + 1, 0:64].rearrange("p g (a b) -> p g a b", b=4); r_ = rc[:, g:g + 1, 0:64].rearrange("p g (a b) -> p g a b", b=4)
            T("dve", lambda e, g=g, o=o, r_=r_: e.tensor_tensor(out=o, in0=sl(srcs[g], g, g + 1, n0, n1), in1=r_, op=ALU.mult), [srcs[g], rc], [Mm])
        if pr:
            T("dve", lambda e: e.tensor_tensor(out=dT[:], in0=Mm[:], in1=PU[:, :, 15:143], op=ALU.subtract), [Mm, PU], [dT])
            T("dve", lambda e: e.tensor_copy(PU[:, :, 0:15], PU[:, :, 128:143]), [PU], [PU])
        else:
            T("dve", lambda e: e.tensor_tensor(out=dT[:, :, 0:64].rearrange("p g (a b) -> p g a b", b=4), in0=Mm[:, :, 0:64].rearrange("p g (a b) -> p g a b", b=4), in1=PUS[:, :, :, 15:19], op=ALU.subtract), [Mm, PUS], [dT])
        pb = ps()
        def _mp(e, pb=pb):
            for g in range(4):
                ins = e.matmul(pb[:, g * 128:(g + 1) * 128], lhsT=w_pool_bf[:, g, :], rhs=dT[:, g, :], start=True, stop=True)
            return ins
        T("pe", _mp, [w_pool_bf, dT], [pb])
        for g in range(4):
            T("act", lambda e, pb=pb, g=g: e.activation(out=ymixT[:, g, :], in_=pb[:, g * 128:(g + 1) * 128], func=AF.Copy, scale=psc[:, g:g + 1]), [pb, psc], [ymixT])
        if pr:
            T("dve", lambda e: e.tensor_tensor_scan(out=Bt[:], data0=ones4[:], data1=L1[:], initial=Bprev[:, 0:1], op0=ALU.mult, op1=ALU.subtract), [ones4, L1, Bprev], [Bt])
            T("dve", lambda e: e.tensor_tensor(out=Gt[:], in0=IG[:], in1=Bt[:], op=ALU.subtract), [IG, Bt], [Gt])
            T("dve", lambda e: e.tensor_tensor_scan(out=Mgt[:], data0=Gt[:], data1=Gt[:], initial=Mprev[:, 0:1], op0=ALU.max, op1=ALU.max), [Gt, Mprev], [Mgt])
        else:
            v3 = lambda b_: b_[:].rearrange("p (a b) -> p a b", b=4)
            T("dve", lambda e: e.tensor_scalar(out=v3(Bt)[:, :, 0:1], in0=v3(L1)[:, :, 0:1], scalar1=-1.0, scalar2=None, op0=ALU.mult), [L1], [Bt])
            for ii in range(1, 4):
                T("dve", lambda e, ii=ii: e.tensor_tensor(out=v3(Bt)[:, :, ii:ii + 1], in0=v3(Bt)[:, :, ii - 1:ii], in1=v3(L1)[:, :, ii:ii + 1], op=ALU.subtract), [Bt, L1], [Bt])
            T("dve", lambda e: e.tensor_tensor(out=Gt[:], in0=IG[:], in1=Bt[:], op=ALU.subtract), [IG, Bt], [Gt])
            T("dve", lambda e: e.tensor_tensor(out=v3(Mgt)[:, :, 0:1], in0=v3(Gt)[:, :, 0:1], in1=m0T_sb[:].unsqueeze(2), op=ALU.max), [Gt, m0T_sb], [Mgt])
            for ii in range(1, 4):
                T("dve", lambda e, ii=ii: e.tensor_tensor(out=v3(Mgt)[:, :, ii:ii + 1], in0=v3(Mgt)[:, :, ii - 1:ii], in1=v3(Gt)[:, :, ii:ii + 1], op=ALU.max), [Mgt, Gt], [Mgt])
            T("dve", lambda e: e.tensor_copy(v3(MgE), v3(Mgt)[:, :, 3:4].broadcast_to([4, 32, 4])), [Mgt], [MgE])
        T("dve", lambda e: e.scalar_tensor_tensor(out=NMt[:], in0=Bt[:], scalar=-1.0, in1=Mgt[:], op0=ALU.mult, op1=ALU.subtract), [Bt, Mgt], [NMt])
        if pr:
            T("dve", lambda e: e.tensor_copy(Bprev[:], Bt[:, 127:128]), [Bt], [Bprev])
            T("dve", lambda e: e.tensor_copy(Mprev[:], Mgt[:, 127:128]), [Mgt], [Mprev])
        pb = ps()
        def _trg(e, pb=pb):
            srcl = [Gt, Mgt, NMt] + ([] if pr else [MgE])
            for n_, s_ in enumerate(srcl):
                ins = e.transpose(out=pb[:, 4 * n_:4 * n_ + 4], in_=s_[:], identity=ident[0:4, 0:4])
            return ins
        T("pe", _trg, [Gt, Mgt, NMt, MgE, ident], [pb])
        ncol = 12 if pr else 16
        T("act", lambda e, pb=pb: e.activation(out=COLS[:, 0:ncol], in_=pb[:, 0:ncol], func=AF.Copy), [pb], [COLS])
        pb = ps()
        def _mb(e, pb=pb):
            for h in range(4):
                ins = e.matmul(pb[:, h * 128:(h + 1) * 128], lhsT=selh[0:4, h * 128:(h + 1) * 128], rhs=Mgt[:], start=True, stop=True)
            return ins
        T("pe", _mb, [selh, Mgt], [pb])
        T("dve", lambda e, pb=pb: e.tensor_copy(MgB[:].rearrange("p h t -> p (h t)"), pb[:, 0:512]), [pb], [MgB])
        if pr:
            prevc = MgprevB[:]; endc = MgB[:, :, 127]; ginref = MgB[:, :, 127]; prevbuf = MgprevB
        else:
            prevc = m0col_sb[:]; ginref = COLS[:, 12:16]; prevbuf = m0col_sb
        T("dve", lambda e: e.tensor_tensor(out=tmp4[:], in0=prevc, in1=COLS[:, 4:8], op=ALU.subtract), [prevbuf, COLS], [tmp4])
        T("dve", lambda e: e.tensor_tensor(out=tmp4b[:], in0=COLS[:, 0:4], in1=ginref, op=ALU.subtract), [COLS, MgB], [tmp4b])
        if pr:
            T("dve", lambda e: e.tensor_tensor(out=tmp4c[:], in0=MgprevB[:], in1=MgB[:, :, 127], op=ALU.subtract), [MgprevB, MgB], [tmp4c])
        T("act", lambda e: e.activation(out=EMT[:], in_=COLS[:, 8:12], func=AF.Exp), [COLS], [EMT])
        T("act", lambda e: e.activation(out=WP[:], in_=tmp4[:], func=AF.Exp), [tmp4], [WP])
        T("act", lambda e: e.activation(out=GIN[:], in_=tmp4b[:], func=AF.Exp), [tmp4b], [GIN])
        if pr:
            T("act", lambda e: e.activation(out=GP[:], in_=tmp4c[:], func=AF.Exp), [tmp4c], [GP])
        else:
            pbm = ps()
            def _m0(e, pbm=pbm):
                for h in range(4):
                    ins = e.matmul(pbm[:, h * 32:(h + 1) * 32], lhsT=selh[0:4, h * 128:(h + 1) * 128], rhs=m0T_sb[:], start=True, stop=True)
                return ins
            T("pe", _m0, [selh, m0T_sb], [pbm])
            T("dve", lambda e, pbm=pbm: e.tensor_copy(m0B[:].rearrange("p h j -> p (h j)"), pbm[:, 0:128]), [pbm], [m0B])
            T("dve", lambda e: e.tensor_tensor(out=GPB[:], in0=m0B[:, :, 0:16], in1=MgB[:].rearrange("p h (j i) -> p h j i", i=4)[:, :, 0:16, 3], op=ALU.subtract), [m0B, MgB], [GPB])
            T("act", lambda e: e.activation(out=GPB[:], in_=GPB[:], func=AF.Exp), [GPB], [GPB])
        mask = maskp if pr else masks
        for h in range(4):
            a_ = Aw[h % 2]; e_ = Ew[h % 2]; pt_ = PT[h % 2]; tn_ = TMPN[h % 2]
            pbS = ps()
            T("pe", lambda e, pbS=pbS, h=h: e.matmul(pbS[:, 0:128], lhsT=kT[:, h, :], rhs=qT[:, h, :], start=True, stop=True), [kT, qT], [pbS])
            T("dve", lambda e, a_=a_, h=h: e.scalar_tensor_tensor(out=a_[:], in0=MgB[:, h, :], scalar=COLS[:, h:h + 1], in1=mask[:], op0=ALU.subtract, op1=ALU.add), [MgB, COLS, mask], [a_])
            T("act", lambda e, a_=a_, e_=e_: e.activation(out=e_[:], in_=a_[:], func=AF.Exp, scale=-1.0), [a_], [e_])
            T("dve", lambda e, pbS=pbS, e_=e_, pt_=pt_: e.tensor_tensor(out=pt_[:], in0=pbS[:, 0:128], in1=e_[:], op=ALU.mult), [pbS, e_], [pt_])
            pbI = ps()
            T("pe", lambda e, pbI=pbI, pt_=pt_, h=h: e.matmul(pbI[:, 0:129], lhsT=pt_[:], rhs=vtok[:, h, :], start=True, stop=True), [pt_, vtok], [pbI])
            pbN = ps()
            if pr:
                T("pe", lambda e, pbN=pbN, h=h: e.matmul(pbN[:, 0:129], lhsT=qT[:, h, :], rhs=Sbf[:, h, :], start=True, stop=True), [qT, Sbf], [pbN])
            else:
                S0h = S0hp[h % 2]
                if h == 0:
                    LD("act", n0all[:], n0fm, n0all)
                    P.dma("sp", lambda e: e.dma_start(out=S0hp[0][:, :, 0:128], in_=S0d[:, 0].rearrange("j d e -> d j e")), reads=[], writes=[S0hp[0]])
                if h + 1 < 4:
                    P.dma("sp", lambda e, h=h: e.dma_start(out=S0hp[(h + 1) % 2][:, :, 0:128], in_=S0d[:, h + 1].rearrange("j d e -> d j e")), reads=[], writes=[S0hp[(h + 1) % 2]])
                T("act", lambda e, h=h, S0h=S0h: e.activation(out=S0h[:, :, 128], in_=n0all[:, h, :], func=AF.Copy), [n0all], [S0h])
                T("act", lambda e, S0h=S0h: e.activation(out=S0bf[:], in_=S0h[:], func=AF.Copy), [S0h], [S0bf])
                T("dve", lambda e, h=h: e.tensor_copy(qTz[:, 0:2112].rearrange("p (j r) -> p j r", r=132)[:, :, 0:4], qT[:, h, 0:64].rearrange("p (j i) -> p j i", i=4)), [qT], [qTz])
                def _mi(e, pbN=pbN):
                    for j in range(16):
                        ins = e.matmul(pbN[:, 0:129], lhsT=qTz[:, j * 128:(j + 1) * 128], rhs=S0bf[:, j, :], start=(j == 0), stop=(j == 15))
                    return ins
                T("pe", _mi, [qTz, S0bf], [pbN])
            T("act", lambda e, pbN=pbN, tn_=tn_, h=h: e.activation(out=tn_[:], in_=pbN[:, 0:129], func=AF.Copy, scale=WP[:, h:h + 1]), [pbN, WP], [tn_])
            T("dve", lambda e, pbI=pbI, tn_=tn_, h=h: e.tensor_tensor(out=ND[:, h, :], in0=pbI[:, 0:129], in1=tn_[:], op=ALU.add), [pbI, tn_], [ND])
            if pr:
                kg = KG[h % 2]
                T("dve", lambda e, kg=kg, h=h: e.tensor_scalar(out=kg[:], in0=ktok[:, h * 128:(h + 1) * 128], scalar1=GIN[:, h:h + 1], scalar2=None, op0=ALU.mult), [ktok, GIN], [kg])
                pbU = ps()
                T("pe", lambda e, pbU=pbU, kg=kg, h=h: e.matmul(pbU[:, 0:129], lhsT=kg[:], rhs=vtok[:, h, :], start=True, stop=True), [kg, vtok], [pbU])
                T("dve", lambda e, pbU=pbU, h=h: e.scalar_tensor_tensor(out=S32[:, h, :], in0=S32[:, h, :], scalar=GP[:, h:h + 1], in1=pbU[:, 0:129], op0=ALU.mult, op1=ALU.add), [S32, GP, pbU], [S32])
            else:
                T("dve", lambda e, h=h: e.tensor_scalar(out=GINM[:], in0=bmask[:], scalar1=GIN[:, h:h + 1], scalar2=None, op0=ALU.mult), [bmask, GIN], [GINM])
                T("dve", lambda e, h=h: e.tensor_tensor(out=KGZ[:], in0=ktok[:, h * 128:(h + 1) * 128].unsqueeze(1).broadcast_to([128, 16, 128]), in1=GINM[:].unsqueeze(2).broadcast_to([128, 16, 128]), op=ALU.mult), [ktok, GINM], [KGZ])
                for j in range(16):
                    pbU = ps()
                    T("pe", lambda e, pbU=pbU, j=j, h=h: e.matmul(pbU[:, 0:129], lhsT=KGZ[:, j, :], rhs=vtok[:, h, :], start=True, stop=True), [KGZ, vtok], [pbU])
                    T("dve", lambda e, pbU=pbU, j=j, h=h: e.scalar_tensor_tensor(out=SN[:, j, :], in0=S0hp[h % 2][:, j, :], scalar=GPB[:, h, j:j + 1], in1=pbU[:, 0:129], op0=ALU.mult, op1=ALU.add), [S0hp[h % 2], GPB, pbU], [SN])
                ST("sp", C_s[:, h].rearrange("j d e -> d j e"), SN[:, :, 0:128], SN)
                pbn = ps()
                T("pe", lambda e, pbn=pbn: e.transpose(out=pbn[0:16, 0:128], in_=SN[:, :, 128], identity=ident[:]), [SN, ident], [pbn])
                T("act", lambda e, pbn=pbn: e.activation(out=NS[:], in_=pbn[0:16, 0:128], func=AF.Copy), [pbn], [NS])
                ST("sp", n_s[:, h, :], NS[:], NS)
        if pr:
            T("act", lambda e: e.activation(out=Sbf[:], in_=S32[:], func=AF.Copy), [S32], [Sbf])
            T("dve", lambda e: e.tensor_copy(MgprevB[:], MgB[:, :, 127]), [MgB], [MgprevB])
            if last_p:
                ST("sp", C_p.rearrange("h d e -> d h e"), S32[:, :, 0:128], S32)
                pbn = ps()
                T("pe", lambda e, pbn=pbn: e.transpose(out=pbn[0:4, 0:128], in_=S32[:, :, 128], identity=ident[:]), [S32, ident], [pbn])
                NPo = sb([4, 128], F32, "NPo"); MPo = sb([4, 1], F32, "MPo")
                T("act", lambda e, pbn=pbn: e.activation(out=NPo[:], in_=pbn[0:4, 0:128], func=AF.Copy), [pbn], [NPo])
                ST("sp", n_p, NPo[:], NPo)
                T("dve", lambda e: e.tensor_scalar(out=MPo[:], in0=NMt[:, 127:128], scalar1=-1.0, scalar2=None, op0=ALU.mult), [NMt], [MPo])
                ST("sp", m_p, MPo[:], MPo)
        else:
            T("dve", lambda e: e.tensor_scalar(out=MSo[:], in0=NMt[:].rearrange("p (j i) -> p j i", i=4)[:, 0:16, 3], scalar1=-1.0, scalar2=None, op0=ALU.mult), [NMt], [MSo])
            ST("sp", m_s, MSo[:], MSo)
        T("dve", lambda e: e.scalar_tensor_tensor(out=absd[:], in0=ND[:, :, 128], scalar=-1.0, in1=ND[:, :, 128], op0=ALU.mult, op1=ALU.max), [ND], [absd])
        T("dve", lambda e: e.tensor_tensor(out=absd[:], in0=absd[:], in1=EMT[:], op=ALU.max), [absd, EMT], [absd])
        T("dve", lambda e: e.reciprocal(out=RD[:], in_=absd[:]), [absd], [RD])
        for h in range(4):
            T("dve", lambda e, h=h: e.tensor_scalar(out=HH[:, h, :], in0=ND[:, h, 0:128], scalar1=RD[:, h:h + 1], scalar2=None, op0=ALU.mult), [ND, RD], [HH])
        for h in range(4):
            T("dve", lambda e, h=h: e.bn_stats(out=st4[:, h, :], in_=HH[:, h, :]), [HH], [st4])
        for h in range(4):
            T("dve", lambda e, h=h: e.bn_aggr(out=MV4[:, h, :], in_=st4[:, h, :]), [st4], [MV4])
        T("act", lambda e: e.activation(out=LN4[:], in_=MV4[:, :, 1], func=AF.Ln, bias=epsc[:], scale=1.0), [MV4, epsc], [LN4])
        T("act", lambda e: e.activation(out=RS4[:], in_=LN4[:], func=AF.Exp, scale=-0.5), [LN4], [RS4])
        pb = ps(); pbv = pb.t[:].bitcast(BF16)
        for h in range(4):
            T("dve", lambda e, h=h: e.tensor_scalar(out=HN[:, h, :], in0=HH[:, h, :], scalar1=MV4[:, h, 0:1], scalar2=RS4[:, h:h + 1], op0=ALU.subtract, op1=ALU.mult), [HH, MV4, RS4], [HN])
        def _trh(e, pbv=pbv):
            for h in range(4):
                ins = e.transpose(out=pbv[:, h * 128:(h + 1) * 128], in_=HN[:, h, :], identity=identb[:])
            return ins
        T("pe", _trh, [HN, identb], [pb])
        for h in range(4):
            T("dve", lambda e, pbv=pbv, h=h: e.scalar_tensor_tensor(out=ymixT[:, 4 + h, :], in0=pbv[:, h * 128:(h + 1) * 128], scalar=mhgs[:, h:h + 1], in1=sgT[:, h, :], op0=ALU.mult, op1=ALU.mult), [pb, mhgs, sgT], [ymixT])
        if DEBUG and pr and i == 0:
            ST("sp", dbg["pu"], PU[:], PU); ST("sp", dbg["qT"], qT[:], qT); ST("sp", dbg["ymixT"], ymixT[:], ymixT); ST("sp", dbg["nd"], ND[:], ND); ST("sp", dbg["cols"], COLS[:], COLS)
        for hf in range(2):
            pb = ps()
            def _mo(e, pb=pb, hf=hf):
                for k in range(8):
                    ins = e.matmul(pb[:, 0:512], lhsT=ymixT[:, k, :], rhs=w_out_bf[:, k, hf * 512:(hf + 1) * 512], start=(k == 0), stop=(k == 7))
                return ins
            T("pe", _mo, [ymixT, w_out_bf], [pb])
            T("dve", lambda e, pb=pb, hf=hf: e.tensor_tensor(out=T2[:, hf * 512:(hf + 1) * 512], in0=pb[:, 0:512], in1=MB[:, 2048 + hf * 512:2048 + (hf + 1) * 512], op=ALU.mult), [pb, MB], [T2])
        T("dve", lambda e: e.scalar_tensor_tensor(out=T2[:], in0=X[:], scalar=ALPHA_, in1=T2[:], op0=ALU.mult, op1=ALU.add), [X, T2], [T2])
        ln_stats(T2); ln_affine(T2, T1, X1, LNG, LNG[:, 0, :], LNG[:, 1, :])
        P.dma("sp", lambda e: e.dma_start(out=x1_dram[ti * 128:(ti + 1) * 128, :], in_=X1[:]), reads=[X1], writes=[x1_dram], sem_buf=x1_dram)
        if not do_peer:
            if pr:
                ST("sp", yp[i * 128:(i + 1) * 128, :], X1[:], X1)
            else:
                ST("sp", ys, X1[0:64, :], X1)

    def stageR(i, pr, pp, p3=0, preloaded=False):
        ti = i if pr else ntiles_prompt
        X1 = X1p[p3]; U2 = U2p[pp]; EIDX = EIDXp[pp]; GG = GGp[pp]; MBB = MBBp if pr else MBBs
        T1 = T1r
        if not preloaded:
            LD("sp", X1[:], x1_dram[ti * 128:(ti + 1) * 128, :], X1, r=[x1_dram])
        ln_stats(X1, LNR); ln_affine(X1, T1, U2, MBB, MBB[:, 1024:2048], MBB[:, 0:1024], LNR)
        for half in range(2):
            pb = ps()
            def _tr2(e, pb=pb, half=half):
                for k in range(4):
                    kk = half * 4 + k
                    ins = e.transpose(out=pb[:, k * 128:(k + 1) * 128], in_=U2[:, kk * 128:(kk + 1) * 128], identity=ident[:])
                return ins
            T("pe", _tr2, [U2, ident], [pb])
            T("act", lambda e, pb=pb, half=half: e.activation(out=U2T[:, half * 4:(half + 1) * 4, :].rearrange("p k t -> p (k t)"), in_=pb[:, 0:512], func=AF.Copy), [pb], [U2T])
        pb2s = {}
        def k1_load(hp):
            wq = wqst[hp % 3]
            LD("sp", wq[:], w_q[hp], wq)
        def k1_front(hp):
            wq = wqst[hp % 3]; qs = qTs[hp % 2]
            pb = ps()
            def _mq(e, pb=pb, wq=wq):
                for k in range(8):
                    ins = e.matmul(pb[:, 0:128], lhsT=wq[:, k, :], rhs=U2T[:, k, :], start=(k == 0), stop=(k == 7))
                return ins
            T("pe", _mq, [wq, U2T], [pb])
            T("act", lambda e, pb=pb, qs=qs: e.activation(out=qs[:], in_=pb[:, 0:128], func=AF.Copy), [pb], [qs])
            pb2 = ps()
            T("pe", lambda e, pb2=pb2, qs=qs, hp=hp: e.matmul(pb2[:, 0:128], lhsT=qs[:], rhs=keys_sb[:, hp, :], start=True, stop=True), [qs, keys_sb], [pb2])
            pb2s[hp] = pb2
        def k1_back(hp):
            pb2 = pb2s[hp]; s2 = SC2[hp % 2]
            T("dve", lambda e, pb2=pb2, hp=hp: e.max(out=sv[:, hp, 0:8], in_=pb2[:, 0:128]), [pb2], [sv])
            T("dve", lambda e, pb2=pb2, hp=hp: e.max_index(out=si[:, hp, 0:8], in_max=sv[:, hp, 0:8], in_values=pb2[:, 0:128]), [pb2, sv], [si])
            T("dve", lambda e, pb2=pb2, hp=hp, s2=s2: e.match_replace(out=s2[:], in_to_replace=sv[:, hp, 0:8], in_values=pb2[:, 0:128], imm_value=-1e30), [pb2, sv], [s2])
            T("dve", lambda e, hp=hp, s2=s2: e.max(out=sv[:, hp, 8:16], in_=s2[:]), [s2], [sv])
            T("dve", lambda e, hp=hp, s2=s2: e.max_index(out=si[:, hp, 8:16], in_max=sv[:, hp, 8:16], in_values=s2[:]), [s2, sv], [si])
        k1_load(0); k1_load(1); k1_front(0)
        for hp in range(16):
            if hp + 2 < 16:
                k1_load(hp + 2)
            if hp + 1 < 16:
                k1_front(hp + 1)
            k1_back(hp)
        T("dve", lambda e: e.tensor_copy(sif[:], si[:]), [si], [sif])
        svv = sv[:].rearrange("p (h q) a -> p h q a", q=2)
        T("dve", lambda e: e.tensor_tensor(out=combA[:].rearrange("p h (a b) -> p h a b", b=16), in0=svv[:, :, 0, :].unsqueeze(3).broadcast_to([128, 8, 16, 16]), in1=svv[:, :, 1, :].unsqueeze(2).broadcast_to([128, 8, 16, 16]), op=ALU.add), [sv], [combA])
        for h in range(8):
            cbuf = combA; c2 = comb2[h % 2]
            cf = combA[:, h, :]
            T("dve", lambda e, h=h, cf=cf: e.max(out=c8[:, h, 0:8], in_=cf), [cbuf], [c8])
            T("dve", lambda e, h=h, cf=cf: e.max_index(out=cpos[:, h, 0:8], in_max=c8[:, h, 0:8], in_values=cf), [cbuf, c8], [cpos])
            T("dve", lambda e, h=h, cf=cf, c2=c2: e.match_replace(out=c2[:], in_to_replace=c8[:, h, 0:8], in_values=cf, imm_value=-1e30), [cbuf, c8], [c2])
            T("dve", lambda e, h=h, c2=c2: e.max(out=c8[:, h, 8:16], in_=c2[:]), [c2], [c8])
            T("dve", lambda e, h=h, c2=c2: e.max_index(out=cpos[:, h, 8:16], in_max=c8[:, h, 8:16], in_values=c2[:]), [c2, c8], [cpos])
        cpf = cpos[:].rearrange("p h k -> p (h k)")
        T("dve", lambda e: e.tensor_single_scalar(out=ca[:], in_=cpf, scalar=4, op=ALU.logical_shift_right), [cpos], [ca])
        T("dve", lambda e: e.tensor_single_scalar(out=cb[:], in_=cpf, scalar=15, op=ALU.bitwise_and), [cpos], [cb])
        T("dve", lambda e: e.tensor_copy(caf[:], ca[:]), [ca], [caf])
        T("dve", lambda e: e.tensor_copy(cbf[:], cb[:]), [cb], [cbf])
        iob = iota16b[:].unsqueeze(1).broadcast_to([128, 128, 16])
        for (cf_, pidx, dst) in ((caf, 0, i1s), (cbf, 1, i2s)):
            T("dve", lambda e, cf_=cf_: e.tensor_tensor(out=oh[:], in0=cf_[:].unsqueeze(2).broadcast_to([128, 128, 16]), in1=iob, op=ALU.is_equal), [cf_, iota16b], [oh])
            sview = sif[:].rearrange("p (h q) a -> p h q a", q=2)[:, :, pidx, :]
            T("dve", lambda e, sview=sview: e.tensor_tensor(out=oh2[:].rearrange("p (h k) a -> p h k a", h=8), in0=oh[:].rearrange("p (h k) a -> p h k a", h=8), in1=sview.unsqueeze(2).broadcast_to([128, 8, 16, 16]), op=ALU.mult), [oh, sif], [oh2])
            T("dve", lambda e, dst=dst: e.tensor_reduce(out=dst[:], in_=oh2[:], axis=AX.X, op=ALU.add), [oh2], [dst])
        T("dve", lambda e: e.scalar_tensor_tensor(out=i1s[:], in0=i1s[:], scalar=128.0, in1=i2s[:], op0=ALU.mult, op1=ALU.add), [i1s, i2s], [i1s])
        T("dve", lambda e: e.tensor_copy(EIDX[:], i1s[:]), [i1s], [EIDX])
        if not pr:
            T("dve", lambda e: e.memset(EIDX[64:128, :], 1 << 30), [], [EIDX])
        T("dve", lambda e: e.tensor_tensor(out=cm[:], in0=c8[:], in1=c8[:, :, 0:1].broadcast_to([128, 8, 16]), op=ALU.subtract), [c8], [cm])
        T("act", lambda e: e.activation(out=ce[:], in_=cm[:], func=AF.Exp), [cm], [ce])
        T("dve", lambda e: e.tensor_reduce(out=csum[:], in_=ce[:], axis=AX.X, op=ALU.add), [ce], [csum])
        T("dve", lambda e: e.reciprocal(out=csum[:], in_=csum[:]), [csum], [csum])
        T("dve", lambda e: e.tensor_tensor(out=GG[:], in0=ce[:], in1=csum[:].unsqueeze(2).broadcast_to([128, 8, 16]), op=ALU.mult), [ce, csum], [GG])
    def stageG(i, pr, pp, dl, p3=0):
        X1 = X1p[p3]; U2 = U2p[pp]; EIDX = EIDXp[pp]; GG = GGp[pp]; MBB = MBBp if pr else MBBs
        nper = (len(dl) + 111) // 112
        GGf = GG[:].rearrange("p h k -> p (h k)")
        def dbuild(sl_):
            d_ = Dg[sl_ % 4]; pz = sl_ % 2; cs = sl_ // 2
            b_ = rb[sl_ % NRB]
            cf = CFp[pz]
            T("act", lambda e, cf=cf, pz=pz, cs=cs, sl_=sl_: e.activation(out=cf[:, cs:cs + 1], in_=GELp[pz][:, cs:cs + 1], func=AF.Copy, scale=GGf[:, sl_:sl_ + 1]), [GELp[pz], GG], [cf])
            T("act", lambda e, d_=d_, cf=cf, cs=cs: e.activation(out=d_[:], in_=identb[:], func=AF.Copy, scale=cf[:, cs:cs + 1]), [identb, cf], [d_])
            def _mv(e, b_=b_, d_=d_, sl_=sl_):
                e.matmul(PSY[0][:, 0:512], lhsT=d_[:], rhs=b_[:, 1024:1536], start=(sl_ == 0), stop=(sl_ == NSLOT - 1))
                return e.matmul(PSY[1][:, 0:512], lhsT=d_[:], rhs=b_[:, 1536:2048], start=(sl_ == 0), stop=(sl_ == NSLOT - 1))
            T("pe", _mv, [b_, d_], [PSY[0], PSY[1]])
        for sl_ in range(NSLOT):
            b_ = rb[sl_ % NRB]; pz = sl_ % 2; cs = sl_ // 2
            if pr:
                P.dma("pool", lambda e, b_=b_, sl_=sl_: e.indirect_dma_start(out=b_[:], out_offset=None, in_=TAB[:, :], in_offset=bass.IndirectOffsetOnAxis(ap=EIDX[:, sl_:sl_ + 1], axis=0)), reads=[EIDX, TAB], writes=[b_])
            else:
                def _gs(e, b_=b_, sl_=sl_):
                    if sl_ == 0:
                        e.reg_mov(bcreg, 16383)
                    return e.indirect_dma_start(out=b_[:], out_offset=None, in_=TAB[:, :], in_offset=bass.IndirectOffsetOnAxis(ap=EIDX[:, sl_:sl_ + 1], axis=0), bounds_check=bcreg, oob_is_err=False)
                P.dma("pool", _gs, reads=[EIDX, TAB], writes=[b_])
            T("dve", lambda e, b_=b_, pz=pz, cs=cs, sl_=sl_: e.scalar_tensor_tensor(out=junkp[sl_ % 4][:], in0=b_[:, 0:1024], scalar=1.0, in1=U2[:], op0=ALU.mult, op1=ALU.mult, accum_out=ACTVp[pz][:, cs:cs + 1]), [b_, U2], [junkp[sl_ % 4], ACTVp[pz]])
            T("act", lambda e, pz=pz, cs=cs: e.activation(out=GELp[pz][:, cs:cs + 1], in_=ACTVp[pz][:, cs:cs + 1], func=AF.Gelu), [ACTVp[pz]], [GELp[pz]])
            if sl_ >= 1:
                dbuild(sl_ - 1)
            for _ in range(nper):
                if dl:
                    dl.pop(0)()
        dbuild(NSLOT - 1)
        while dl:
            dl.pop(0)()
        for hf in range(2):
            T("dve", lambda e, hf=hf: e.tensor_tensor(out=T2[:, hf * 512:(hf + 1) * 512], in0=PSY[hf][:, 0:512], in1=MBB[:, 2048 + hf * 512:2048 + (hf + 1) * 512], op=ALU.mult), [PSY[hf], MBB], [T2])
        T("dve", lambda e: e.scalar_tensor_tensor(out=T2[:], in0=X1[:], scalar=ALPHA_, in1=T2[:], op0=ALU.mult, op1=ALU.add), [X1, T2], [T2])
        T1 = T1g
        ln_stats(T2); ln_affine(T2, T1, OUT, LNG2, LNG2[:, 0, :], LNG2[:, 1, :])
        if pr:
            ST("sp", yp[i * 128:(i + 1) * 128, :], OUT[:], OUT)
        else:
            ST("sp", ys, OUT[0:64, :], OUT)

    for i in range(ntiles_prompt):
        tile(i, True)
    if do_sample:
        T("dve", lambda e: e.memset(dT[:], 0.0), [], [dT])
        for half in range(2):
            for ii in range(4):
                LD("sp", MB[half * 64 + ii:half * 64 + 64:4, :], mod_dram[1:17, 0:3072], MB, r=[mod_dram])
        LD("sp", PUS[:, :, :, 0:15], spool, PUS)
        dummy = sb([1, 1], F32, "dummyb")
        P.dma("act", lambda e: e.dma_start(out=pool_s[:, 0:11, :], in_=spool_raw[:, 4:15, :]), reads=[], writes=[dummy], sem_buf=dummy, is_output=True)
        tile(0, False)
    P.end_phase()
    if do_peer:
        P.begin_phase()
        ident = sb([128, 128], F32, "identB"); identb = sb([128, 128], BF16, "identbB"); iota16 = sb([128, 16], F32, "iota16B")
        LD("sp", ident[:], c_ident, ident); LD("sp", iota16[:], c_iota, iota16)
        T("act", lambda e: e.activation(out=identb[:], in_=ident[:], func=AF.Copy), [ident], [identb])
        epsc = sb([128, 1], F32, "epscB")
        T("dve", lambda e: e.memset(epsc[:], EPS_), [], [epsc])
        stats = sb([128, 2, 6], F32, "statsB"); mv = sb([128, 2], F32, "mvB"); lnv = sb([128, 1], F32, "lnvB"); rstd = sb([128, 1], F32, "rstdB")
        LNG2 = sb([128, 2, DM], F32, "LNG2")
        for j, a in enumerate((ln2g, ln2b)):
            LD("act", LNG2[:, j, :], a.broadcast_to([128, DM]), LNG2)
        keys_sb = sb([128, 16, 128], F32, "keys_sb")
        LD("act", keys_sb[:], keysT, keys_sb)
        MBBp = sb([128, 3072], F32, "MBBp"); MBBs = sb([128, 3072], F32, "MBBs")
        LD("sp", MBBp[:], mod_dram[0:1, 3072:6144].broadcast_to([128, 3072]), MBBp, r=[mod_dram])
        for half in range(2):
            for ii in range(4):
                LD("act", MBBs[half * 64 + ii:half * 64 + 64:4, :], mod_dram[1:17, 3072:6144], MBBs, r=[mod_dram])
        X1p = [sb([128, DM], F32, f"X1B{i}") for i in range(3)]; T1r = sb([128, DM], F32, "T1r"); T1g = sb([128, DM], F32, "T1g"); T2 = sb([128, DM], F32, "T2B")
        U2p = [sb([128, DM], F32, f"U2{i}") for i in range(2)]; U2T = sb([128, 8, 128], F32, "U2T"); OUT = sb([128, DM], F32, "OUT")
        LNR = (sb([128, 2, 6], F32, "statsR"), sb([128, 2], F32, "mvR"), sb([128, 1], F32, "lnvR"), sb([128, 1], F32, "rstdR"))
        LNSET[0] = (stats, mv, lnv, rstd)
        CFp = [sb([128, 64], F32, f"CF{i}") for i in range(2)]
        wqst = [sb([128, 8, 128], F32, f"wqst{i}") for i in range(3)]
        qTs = [sb([128, 128], F32, f"qTs{i}") for i in range(2)]
        SC2 = [sb([128, 128], F32, f"SC2{i}") for i in range(2)]
        sv = sb([128, 16, 16], F32, "sv"); si = sb([128, 16, 16], U32, "si"); sif = sb([128, 16, 16], BF16, "sif")
        combA = sb([128, 8, 256], F32, "combA"); comb2 = [sb([128, 256], F32, f"comb2{i}") for i in range(2)]
        c8 = sb([128, 8, 16], F32, "c8"); cpos = sb([128, 8, 16], U32, "cpos"); ca = sb([128, 128], U32, "ca"); cb = sb([128, 128], U32, "cb")
        caf = sb([128, 128], BF16, "caf"); cbf = sb([128, 128], BF16, "cbf")
        oh = sb([128, 128, 16], BF16, "oh"); oh2 = sb([128, 128, 16], BF16, "oh2"); iota16b = sb([128, 16], BF16, "iota16b")
        T("act", lambda e: e.activation(out=iota16b[:], in_=iota16[:], func=AF.Copy), [iota16], [iota16b])
        i1s = sb([128, 128], F32, "i1s"); i2s = sb([128, 128], F32, "i2s"); EIDXp = [sb([128, 128], I32, f"EIDX{i}") for i in range(2)]
        cm = sb([128, 8, 16], F32, "cm"); ce = sb([128, 8, 16], F32, "ce"); csum = sb([128, 8], F32, "csum"); GGp = [sb([128, 8, 16], F32, f"GG{i}") for i in range(2)]
        ACTVp = [sb([128, 64], F32, f"ACTV{i}") for i in range(2)]; GELp = [sb([128, 64], F32, f"GEL{i}") for i in range(2)]
        NRB = 16
        bcreg = nc.gpsimd.alloc_register("bcreg")
        rb = [sb([128, 2048], BF16, f"rb{i}") for i in range(NRB)]
        junkp = [sb([128, DM], BF16, f"junk{i}") for i in range(4)]
        Dg = [sb([128, 128], BF16, f"Dg{i}") for i in range(4)]

        tl = [(i, True) for i in range(ntiles_prompt)] + ([(0, False)] if do_sample else [])
        def tix(n_):
            return tl[n_][0] if tl[n_][1] else ntiles_prompt
        stageR(tl[0][0], tl[0][1], 0, 0)
        if len(tl) > 1:
            LD("sp", X1p[1][:], x1_dram[tix(1) * 128:(tix(1) + 1) * 128, :], X1p[1], r=[x1_dram])
        for n_, (i, pr) in enumerate(tl):
            dl = []
            if n_ + 2 < len(tl):
                LD("sp", X1p[(n_ + 2) % 3][:], x1_dram[tix(n_ + 2) * 128:(tix(n_ + 2) + 1) * 128, :], X1p[(n_ + 2) % 3], r=[x1_dram])
            if n_ + 1 < len(tl):
                defer[0] = dl
                stageR(tl[n_ + 1][0], tl[n_ + 1][1], (n_ + 1) % 2, (n_ + 1) % 3, preloaded=True)
                defer[0] = None
            stageG(i, pr, n_ % 2, dl, n_ % 3)
    P.finish()
    return nc


def _consts():
    c = {}
    c["c_ident"] = np.eye(128, dtype=np.float32)
    s = np.arange(128)[:, None]; t = np.arange(128)[None, :]
    c["c_maskp"] = np.where(s <= t, 0.0, 1e4).astype(np.float32)
    c["c_masks"] = np.where((s <= t) & (s // 4 == t // 4), 0.0, 1e4).astype(np.float32)
    bm = np.zeros((128, 16), np.float32)
    for tt in range(64):
        bm[tt, tt // 4] = 1.0
    c["c_bmask"] = bm
    sel = np.zeros((4, 512), np.float32)
    for h in range(4):
        sel[h, h * 128:(h + 1) * 128] = 1.0
    c["c_selh"] = sel
    c["c_iota"] = np.tile(np.arange(16, dtype=np.float32)[None, :], (128, 1))
    rc0 = np.zeros((128, 4, 128), np.float32); rcw = np.zeros((128, 4, 128), np.float32)
    for g, w in enumerate((2, 4, 8, 16)):
        rc0[:, g, :] = 1.0 / np.minimum(np.arange(128) + 1, w)
        rcw[:, g, :] = 1.0 / w
    c["c_rc0"] = rc0; c["c_rcw"] = rcw
    return c


def kernel(**inp):
    f = lambda a: np.ascontiguousarray(np.asarray(a), dtype=np.float32)
    g = {k: f(v) for k, v in inp.items()}
    b_in = g["b_in"][0]
    shared = {
        "w_mod": g["w_mod"][0], "b_mod": g["b_mod"][0][None, :], "w_in": g["w_in"][0],
        "b_fm": f(b_in[:2560].reshape(20, 128).T), "b_gate": f(np.stack([b_in[2560:2564], b_in[2564:2568]], axis=1)),
        "b_row": b_in[None, :], "w_pool": g["w_pool"][0], "pscale": f(g["pool_scale"][0].reshape(4, 128).T),
        "mhg": f(g["mh_norm_g"][0].reshape(4, 128).T), "w_out": g["w_out"][0],
        "ln1g": g["ln1_g"][0][None, :], "ln1b": g["ln1_b"][0][None, :], "ln2g": g["ln2_g"][0][None, :], "ln2b": g["ln2_b"][0][None, :],
        "w_q": f(g["w_q"][0].reshape(8, 128, 16, 128).transpose(2, 1, 0, 3)), "keysT": f(g["sub_keys"][0].reshape(16, 128, 128).transpose(2, 0, 1)),
        "u_tab": g["u_tab"][0], "v_tab": g["v_tab"][0],
    }
    shared.update(_consts())
    in_maps = []
    for b in range(8):
        sl = slice(16 * b, 16 * b + 16)
        m = dict(shared)
        m["xp"] = g["x_prompt"][b]
        m["xs"] = f(np.concatenate([g["x_sample"][sl].reshape(64, 1024), np.zeros((64, 1024), np.float32)], 0))
        m["cc"] = f(np.concatenate([g["c_prompt"][b:b + 1], g["c_sample"][sl]], 0))
        sp = g["state_pool"][0, sl]
        m["spool"] = f(sp.reshape(16, 15, 4, 128).transpose(3, 2, 0, 1))
        m["spool_raw"] = f(sp)
        m["S0"] = f(g["state_C"][0, sl])
        m["n0fm"] = f(g["state_n"][0, sl].transpose(2, 1, 0))
        sm = g["state_m"][0, sl]
        m0T = np.zeros((4, 32), np.float32); m0T[:, :16] = sm.T
        m0c = np.zeros((128, 4), np.float32); m0c[:64] = np.repeat(sm, 4, axis=0)
        m["m0T"] = m0T; m["m0col"] = m0c
        in_maps.append(m)
    nc = build_program()
    res = run_bass_kernel_spmd(nc, in_maps, core_ids=list(range(8)))
    R = res.results
    cat = lambda k: np.stack([np.asarray(r[k], dtype=np.float32) for r in R])
    y_p = cat("yp")
    y_s = cat("ys").reshape(128, 4, 1024)
    pool_p = cat("pool_p")[None]
    C_p = cat("C_p")[None]
    n_p = cat("n_p")[None]
    m_p = cat("m_p").reshape(8, 4)[None]
    pool_s = cat("pool_s").reshape(128, 15, 512)[None]
    C_s = cat("C_s").reshape(128, 4, 128, 128)[None]
    n_s = cat("n_s").reshape(128, 4, 128)[None]
    m_s = np.concatenate([np.asarray(r["m_s"], dtype=np.float32).T for r in R], 0)[None]
    return (y_p, y_s, pool_p, C_p, n_p, m_p, pool_s, C_s, n_s, m_s)
```

```python
import numpy as np
from concourse.bass_utils import run_bass_kernel_spmd
import concourse.bass as bass
import concourse.mybir as mybir
from contextlib import ExitStack

F32 = mybir.dt.float32
BF16 = mybir.dt.bfloat16
U32 = mybir.dt.uint32
I32 = mybir.dt.int32
AF = mybir.ActivationFunctionType
ALU = mybir.AluOpType
AX = mybir.AxisListType

ENGS = ("pe", "dve", "act", "pool", "sp")
STRICT_SYNC = True


class Buf:
    def __init__(self, prog, t, name):
        self.p = prog
        self.t = t
        self.name = name
        self.lw = None
        self.rd = {}
        self.dsem = None
        self.dcnt = 0

    def __getitem__(self, k):
        return self.t[k]

    def ap(self):
        return self.t[:]


class Prog:
    def __init__(self, nc):
        self.nc = nc
        self.es = ExitStack()
        self.streams = {e: [] for e in ENGS}
        self.sem = {}
        for e in ENGS:
            self.sem[e] = nc.alloc_semaphore(name=f"sem_{e}")
        self.cnt = {e: 0 for e in ENGS}
        self.known = {e: {} for e in ENGS}
        self.out_waits = []
        self.nbuf = 0
        self.cur = self.es
        self.dbufs = []

    def sbuf(self, shape, dt, name=None):
        self.nbuf += 1
        name = name or f"sb{self.nbuf}"
        t = self.cur.enter_context(self.nc.sbuf_tensor(name, list(shape), dt))
        return Buf(self, t, name)

    def psum(self, shape, dt, name=None):
        self.nbuf += 1
        name = name or f"ps{self.nbuf}"
        t = self.es.enter_context(self.nc.psum_tensor(name, list(shape), dt))
        return Buf(self, t, name)

    def dram(self, name, shape, dt, kind="Internal"):
        t = self.nc.dram_tensor(name, list(shape), dt, kind=kind).ap()
        return Buf(self, t, name)

    def _deps(self, eng, reads, writes, is_dma=False):
        deps = {}

        def add(d, same_ok):
            if d is None:
                return
            kind, k, v = d
            if kind == "e" and k == eng and same_ok and not is_dma and (not STRICT_SYNC or eng == "pe"):
                return
            key = (kind, k)
            if deps.get(key, 0) < v:
                deps[key] = v

        for r in reads:
            add(r.lw, same_ok=(eng == "pe" or eng == "sp"))
        for w in writes:
            add(w.lw, same_ok=True)
            for d in w.rd.values():
                add(d, same_ok=True)
        waits = []
        kn = self.known[eng]
        for key, v in deps.items():
            if kn.get(key, 0) < v:
                kn[key] = v
                s = self.sem[key[1]] if key[0] == "e" else key[1]
                waits.append((s, v))
        return waits

    def op(self, eng, fn, reads=(), writes=()):
        waits = self._deps(eng, reads, writes)
        self.cnt[eng] += 1
        c = self.cnt[eng]
        sem = self.sem[eng]

        def emit(e, waits=waits, fn=fn, sem=sem):
            for s, v in waits:
                e.wait_ge(s, v)
            ins = fn(e)
            ins.then_inc(sem, 1)

        self.streams[eng].append(emit)
        tag = ("e", eng, c)
        for w in writes:
            w.lw = tag
            w.rd = {}
        for r in reads:
            if r not in writes:
                r.rd[("e", eng)] = tag
        return tag

    def dma(self, eng, fn, reads=(), writes=(), sem_buf=None, is_output=False):
        waits = self._deps(eng, reads, writes, is_dma=True)
        sb = sem_buf or (writes[0] if writes else reads[0])
        if sb.dsem is None:
            sb.dsem = self.nc.alloc_semaphore(name=f"dsem_{sb.name}")
            self.dbufs.append(sb)
        sb.dcnt += 16
        val = sb.dcnt
        dsem = sb.dsem

        def emit(e, waits=waits, fn=fn, dsem=dsem):
            for s, v in waits:
                e.wait_ge(s, v)
            ins = fn(e)
            ins.then_inc(dsem, 16)

        self.streams[eng].append(emit)
        tag = ("d", dsem, val)
        for w in writes:
            w.lw = tag
            w.rd = {}
        for r in reads:
            if r not in writes:
                r.rd[("d", dsem)] = tag
        if is_output:
            self.out_waits.append((dsem, val))
        return tag

    def begin_phase(self):
        self.cur = ExitStack()

    def end_phase(self):
        cnts = dict(self.cnt)
        dm = [(b.dsem, b.dcnt) for b in self.dbufs]
        for en in ENGS:
            def emit_bar(e, en=en, cnts=cnts, dm=dm):
                for e2 in ENGS:
                    if e2 != en and cnts[e2] > 0:
                        e.wait_ge(self.sem[e2], cnts[e2])
                for s_, v_ in dm:
                    e.wait_ge(s_, v_)
            self.streams[en].append(emit_bar)
            for e2 in ENGS:
                self.known[en][("e", e2)] = max(self.known[en].get(("e", e2), 0), cnts[e2])
            for s_, v_ in dm:
                self.known[en][("d", s_)] = max(self.known[en].get(("d", s_), 0), v_)
        self._emit_block()
        self.streams = {e: [] for e in ENGS}
        self.cur.close()
        self.cur = self.es

    def _emit_block(self):
        nc = self.nc
        with nc.Block() as block:
            @block.sync
            def _(e):
                for f in self.streams["sp"]:
                    f(e)

            @block.tensor
            def _(e):
                for f in self.streams["pe"]:
                    f(e)

            @block.vector
            def _(e):
                for f in self.streams["dve"]:
                    f(e)

            @block.scalar
            def _(e):
                for f in self.streams["act"]:
                    f(e)

            @block.gpsimd
            def _(e):
                for f in self.streams["pool"]:
                    f(e)

    def finish(self):
        nc = self.nc
        finals = {}
        for s, v in self.out_waits:
            finals[s] = max(finals.get(s, 0), v)
        fin = list(finals.items())

        def emit_final(e, fin=fin):
            for s, v in fin:
                e.wait_ge(s, v)

        self.streams["sp"].append(emit_final)
        self._emit_block()
        self.es.close()

NTP = 16
DM = 1024
ALPHA_ = 2.0 ** 0.25
EPS_ = 1e-5
KS_ = 128.0 ** -0.5
NSLOT = 128
DEBUG = False


def build_program(ntiles_prompt=NTP, do_sample=True, do_peer=True):
    nc = bass.Bass("TRN2", target_bir_lowering=False)

    def din(name, shape, dt=F32):
        return nc.dram_tensor(name, list(shape), dt, kind="ExternalInput").ap()

    def dout(name, shape, dt=F32):
        return nc.dram_tensor(name, list(shape), dt, kind="ExternalOutput").ap()

    xp = din("xp", [2048, DM]); xs = din("xs", [128, DM]); cc = din("cc", [17, DM])
    w_mod = din("w_mod", [DM, 6144]); b_mod = din("b_mod", [1, 6144])
    w_in = din("w_in", [DM, 2568]); b_fm = din("b_fm", [128, 20]); b_gate = din("b_gate", [4, 2])
    b_row = din("b_row", [1, 2568]); w_pool = din("w_pool", [4, 128, 128])
    pscale = din("pscale", [128, 4]); mhg = din("mhg", [128, 4]); w_out = din("w_out", [DM, DM])
    ln1g = din("ln1g", [1, DM]); ln1b = din("ln1b", [1, DM]); ln2g = din("ln2g", [1, DM]); ln2b = din("ln2b", [1, DM])
    w_q = din("w_q", [16, 128, 8, 128]); keysT = din("keysT", [128, 16, 128])
    u_tab = din("u_tab", [16384, DM]); v_tab = din("v_tab", [16384, DM])
    spool = din("spool", [128, 4, 16, 15]); spool_raw = din("spool_raw", [16, 15, 512]); S0d = din("S0", [16, 4, 128, 128]); n0fm = din("n0fm", [128, 4, 16])
    m0T = din("m0T", [4, 32]); m0col = din("m0col", [128, 4])
    c_ident = din("c_ident", [128, 128]); c_maskp = din("c_maskp", [128, 128]); c_masks = din("c_masks", [128, 128])
    c_bmask = din("c_bmask", [128, 16]); c_selh = din("c_selh", [4, 512]); c_iota = din("c_iota", [128, 16])
    c_rc0 = din("c_rc0", [128, 4, 128]); c_rcw = din("c_rcw", [128, 4, 128])

    yp = dout("yp", [2048, DM]); ys = dout("ys", [64, DM]); pool_p = dout("pool_p", [15, 512])
    C_p = dout("C_p", [4, 128, 128]); n_p = dout("n_p", [4, 128]); m_p = dout("m_p", [4, 1])
    pool_s = dout("pool_s", [16, 15, 512]); C_s = dout("C_s", [16, 4, 128, 128]); n_s = dout("n_s", [16, 4, 128])
    m_s = dout("m_s", [4, 16])

    P = Prog(nc)
    mod_dram = P.dram("mod_scr", [17, 6144], F32)
    x1_dram = P.dram("x1_scr", [17 * 128, DM], F32)
    TAB = P.dram("tab_scr", [16384, 2048], BF16)
    P.begin_phase()
    sb = P.sbuf

    defer = [None]

    def T(eng, fn, r, w):
        if defer[0] is not None:
            defer[0].append(lambda: P.op(eng, fn, reads=r, writes=w))
            return None
        return P.op(eng, fn, reads=r, writes=w)

    def LD(eng, out_ap, in_ap, w, r=()):
        if defer[0] is not None:
            defer[0].append(lambda: P.dma(eng, lambda e: e.dma_start(out=out_ap, in_=in_ap), reads=list(r), writes=[w]))
            return None
        return P.dma(eng, lambda e: e.dma_start(out=out_ap, in_=in_ap), reads=list(r), writes=[w])

    def ST(eng, out_ap, in_ap, r, final=True):
        return P.dma(eng, lambda e: e.dma_start(out=out_ap, in_=in_ap), reads=[r], writes=[], sem_buf=r, is_output=final)

    PSB = [P.psum([128, 512], F32, f"psb{i}") for i in range(8)]
    rot = [0]

    def ps():
        b = PSB[rot[0] % 6]
        rot[0] += 1
        return b
    PSY = [PSB[6], PSB[7]]

    ident = sb([128, 128], F32, "ident"); identb = sb([128, 128], BF16, "identb")
    maskp = sb([128, 128], F32, "maskp"); masks = sb([128, 128], F32, "masks")
    bmask = sb([128, 16], F32, "bmask"); selh = sb([4, 512], F32, "selh"); iota16 = sb([128, 16], F32, "iota16")
    rc0 = sb([128, 4, 128], F32, "rc0"); rcw = sb([128, 4, 128], F32, "rcw")
    LD("sp", ident[:], c_ident, ident); LD("sp", maskp[:], c_maskp, maskp); LD("sp", masks[:], c_masks, masks)
    LD("sp", bmask[:], c_bmask, bmask); LD("sp", selh[:], c_selh, selh); LD("sp", iota16[:], c_iota, iota16)
    LD("sp", rc0[:], c_rc0, rc0); LD("sp", rcw[:], c_rcw, rcw)
    T("act", lambda e: e.activation(out=identb[:], in_=ident[:], func=AF.Copy), [ident], [identb])
    epsc = sb([128, 1], F32, "epsc"); onec = sb([128, 1], F32, "onec")
    T("dve", lambda e: e.memset(epsc[:], EPS_), [], [epsc])
    T("dve", lambda e: e.memset(onec[:], 1.0), [], [onec])
    ones4 = sb([4, 128], F32, "ones4")
    T("dve", lambda e: e.memset(ones4[:], 1.0), [], [ones4])
    ones_bf = sb([1, 128], BF16, "ones_bf")
    T("dve", lambda e: e.memset(ones_bf[:], 1.0), [], [ones_bf])

    bfm = sb([128, 20], F32, "bfm"); bgate = sb([4, 2], F32, "bgate"); psc = sb([128, 4], F32, "psc"); mhgs = sb([128, 4], F32, "mhgs")
    LD("sp", bfm[:], b_fm, bfm); LD("sp", bgate[:], b_gate, bgate); LD("sp", psc[:], pscale, psc); LD("sp", mhgs[:], mhg, mhgs)
    bks = sb([128, 4], F32, "bks"); nbg = sb([4, 1], F32, "nbg")
    T("dve", lambda e: e.tensor_scalar(out=bks[:], in0=bfm[:, 8:12], scalar1=KS_, scalar2=None, op0=ALU.mult), [bfm], [bks])
    T("dve", lambda e: e.tensor_scalar(out=nbg[:], in0=bgate[:, 1:2], scalar1=-1.0, scalar2=None, op0=ALU.mult), [bgate], [nbg])
    LNG = sb([128, 2, DM], F32, "LNG")

    Xp = [sb([128, DM], F32, f"Xin{i}") for i in range(2)]; X = Xp[0]; T1 = sb([128, DM], F32, "T1"); T2 = sb([128, DM], F32, "T2"); X1 = sb([128, DM], F32, "X1")
    U1b = sb([128, DM], BF16, "U1b"); U1T = sb([128, 8, 128], BF16, "U1T")
    U1b_f = sb([128, 256], F32, "U1b_f"); OUTA = sb([128, 256], F32, "OUTA")
    stats = sb([128, 2, 6], F32, "stats"); mv = sb([128, 2], F32, "mv"); lnv = sb([128, 1], F32, "lnv"); rstd = sb([128, 1], F32, "rstd")
    PU = sb([128, 4, 143], F32, "PU"); Wa = sb([128, 4, 143], F32, "Wa"); Wb = sb([128, 4, 143], F32, "Wb")
    PUS = sb([128, 4, 16, 19], F32, "PUS"); WaS = sb([128, 4, 16, 19], F32, "WaS"); WbS = sb([128, 4, 16, 19], F32, "WbS")
    Mm = sb([128, 4, 128], F32, "Mm"); dT = sb([128, 4, 128], BF16, "dT")
    qT = sb([128, 4, 128], BF16, "qT"); kT = sb([128, 4, 128], BF16, "kT"); sgT = sb([128, 4, 128], BF16, "sgT")
    ktok = sb([128, 512], BF16, "ktok"); vtok = sb([128, 4, 129], BF16, "vtok"); PUT = sb([128, 512], F32, "PUT")
    ymixT = sb([128, 8, 128], BF16, "ymixT")
    IG = sb([4, 128], F32, "IG"); E1 = sb([4, 128], F32, "E1"); L1 = sb([4, 128], F32, "L1")
    Bt = sb([4, 128], F32, "Bt"); Gt = sb([4, 128], F32, "Gt"); Mgt = sb([4, 128], F32, "Mgt"); NMt = sb([4, 128], F32, "NMt"); MgE = sb([4, 128], F32, "MgE")
    Bprev = sb([4, 1], F32, "Bprev"); Mprev = sb([4, 1], F32, "Mprev")
    COLS = sb([128, 16], F32, "COLS"); MgB = sb([128, 4, 128], F32, "MgB"); MgprevB = sb([128, 4], F32, "MgprevB")
    tmp4 = sb([128, 4], F32, "tmp4"); tmp4b = sb([128, 4], F32, "tmp4b"); tmp4c = sb([128, 4], F32, "tmp4c"); WP = sb([128, 4], F32, "WP"); EMT = sb([128, 4], F32, "EMT"); GIN = sb([128, 4], F32, "GIN"); GP = sb([128, 4], F32, "GP")
    Aw = [sb([128, 128], F32, f"Aw{i}") for i in range(2)]; Ew = [sb([128, 128], F32, f"Ew{i}") for i in range(2)]
    PT = [sb([128, 128], BF16, f"PT{i}") for i in range(2)]
    TMPN = [sb([128, 129], F32, f"TMPN{i}") for i in range(2)]; ND = sb([128, 4, 129], F32, "ND")
    absd = sb([128, 4], F32, "absd"); RD = sb([128, 4], F32, "RD")
    HH = sb([128, 4, 128], F32, "HH"); st4 = sb([128, 4, 6], F32, "st4"); MV4 = sb([128, 4, 2], F32, "MV4"); LN4 = sb([128, 4], F32, "LN4"); RS4 = sb([128, 4], F32, "RS4")
    HN = sb([128, 4, 128], BF16, "HN")
    KG = [sb([128, 128], BF16, f"KG{i}") for i in range(2)]
    S32 = sb([128, 4, 129], F32, "S32"); Sbf = sb([128, 4, 129], BF16, "Sbf")
    for tz in (S32, Sbf, MgprevB, Bprev, Mprev, PU, dT, Mm):
        T("dve", lambda e, tz=tz: e.memset(tz[:], 0.0), [], [tz])
    T("pool", lambda e: e.memset(vtok[:, :, 128:129], 1.0), [], [vtok])

    if True:
        m0T_sb = sb([4, 32], F32, "m0T_sb"); m0col_sb = sb([128, 4], F32, "m0col_sb"); m0B = sb([128, 4, 32], F32, "m0B")
        LD("sp", m0T_sb[:], m0T, m0T_sb); LD("sp", m0col_sb[:], m0col, m0col_sb)
        qTz = sb([128, 16 * 132], BF16, "qTz")
        T("pool", lambda e: e.memset(qTz[:], 0.0), [], [qTz])
        S0h = sb([128, 16, 129], F32, "S0h"); S0h2 = sb([128, 16, 129], F32, "S0h2"); n0all = sb([128, 4, 16], F32, "n0all"); S0bf = sb([128, 16, 129], BF16, "S0bf"); SN = sb([128, 16, 129], F32, "SN")
        GINM = sb([128, 16], F32, "GINM"); KGZ = sb([128, 16, 128], BF16, "KGZ"); GPB = sb([128, 4, 16], F32, "GPB")
        NS = sb([16, 128], F32, "NS"); MSo = sb([4, 16], F32, "MSo")

    w_in_bf = sb([128, 8, 2568], BF16, "w_in_bf"); w_out_bf = sb([128, 8, DM], BF16, "w_out_bf")
    w_pool_bf = sb([128, 4, 128], BF16, "w_pool_bf"); brow_bf = sb([1, 2568], BF16, "brow_bf")
    stgb = [S0h, SN]
    S0hp = [S0h, S0h2]
    cast_eng = ["dve", "act"]
    nci = [0]

    def cast(out_ap, in_ap, r, w):
        en = cast_eng[nci[0] % 2]
        nci[0] += 1
        if en == "act":
            T("act", lambda e: e.activation(out=out_ap, in_=in_ap, func=AF.Copy), r, w)
        else:
            T(en, lambda e: e.tensor_copy(out_ap, in_ap), r, w)
    def fl(b_):
        t_ = b_[:]
        return t_ if len(t_.shape) == 2 else t_.rearrange("p a b -> p (a b)")
    for k in range(8):
        P.dma("pool", lambda e, k=k: e.dma_start(out=w_in_bf[:, k, :], in_=w_in[k * 128:(k + 1) * 128, :]), reads=[], writes=[], sem_buf=w_in_bf)
    w_in_bf.lw = ("d", w_in_bf.dsem, w_in_bf.dcnt)
    for k in range(8):
        P.dma("pool", lambda e, k=k: e.dma_start(out=w_out_bf[:, k, :], in_=w_out[k * 128:(k + 1) * 128, :]), reads=[], writes=[], sem_buf=w_out_bf)
    w_out_bf.lw = ("d", w_out_bf.dsem, w_out_bf.dcnt)
    P.dma("pool", lambda e: e.dma_start(out=w_pool_bf[:], in_=w_pool.rearrange("g c d -> c g d")), reads=[], writes=[w_pool_bf])
    P.dma("pool", lambda e: e.dma_start(out=brow_bf[:], in_=b_row), reads=[], writes=[brow_bf])

    cc_sb = X1; scT = T1
    LD("sp", cc_sb[0:17, :], cc, cc_sb)
    pb = ps()
    def _tr(e, pb=pb):
        for k in range(8):
            ins = e.transpose(out=pb[:, k * 17:(k + 1) * 17], in_=cc_sb[0:17, k * 128:(k + 1) * 128], identity=ident[0:17, 0:17])
        return ins
    T("pe", _tr, [cc_sb, ident], [pb])
    T("act", lambda e, pb=pb: e.activation(out=scT[:, 0:136], in_=pb[:, 0:136], func=AF.Silu), [pb], [scT])
    w_mod_v = w_mod.rearrange("(k p) c -> p k c", p=128)
    MB = sb([128, 3072], F32, "MB")
    stg4 = [S0h, SN, MB, LNG]
    bm3 = [T2, U1b_f, X1]; mc3 = [Xp[0], OUTA, Xp[1]]
    for c in range(24):
        wmb = stg4[c % 4]; wm = fl(wmb)[:, 0:2048].rearrange("p (k c) -> p k c", k=8)
        bmb = bm3[c % 3]; mcb = mc3[c % 3]
        bm = bmb[0:17, 0:256]; mc = mcb[0:17, 0:256]
        LD("sp", wm, w_mod_v[:, :, c * 256:(c + 1) * 256], wmb)
        LD("sp", bm, b_mod[0:1, c * 256:(c + 1) * 256].broadcast_to([17, 256]), bmb)
        pb = ps()
        def _mm(e, pb=pb, wm=wm):
            for k in range(8):
                ins = e.matmul(pb[0:17, 0:256], lhsT=scT[:, k * 17:(k + 1) * 17], rhs=wm[:, k, :], start=(k == 0), stop=(k == 7))
            return ins
        T("pe", _mm, [scT, wmb], [pb])
        plus1 = 1.0 if (c // 4) in (1, 2, 4, 5) else 0.0
        T("dve", lambda e, pb=pb, bm=bm, mc=mc, plus1=plus1: e.scalar_tensor_tensor(out=mc, in0=pb[0:17, 0:256], scalar=plus1, in1=bm, op0=ALU.add, op1=ALU.add), [pb, bmb], [mcb])
        P.dma("act", lambda e, mc=mc, c=c: e.dma_start(out=mod_dram[:, c * 256:(c + 1) * 256], in_=mc), reads=[mcb], writes=[mod_dram], sem_buf=mod_dram)
    LD("sp", MB[:], mod_dram[0:1, 0:3072].broadcast_to([128, 3072]), MB, r=[mod_dram])
    for j, a in enumerate((ln1g, ln1b)):
        LD("act", LNG[:, j, :], a.broadcast_to([128, DM]), LNG)
    NCV = 16
    RCV = 16384 // NCV
    for c in range(NCV):
        for ti_, src in enumerate((u_tab, v_tab)):
            P.dma("pool", lambda e, c=c, ti_=ti_, src=src: e.dma_start(out=TAB[c * RCV:(c + 1) * RCV, ti_ * 1024:(ti_ + 1) * 1024], in_=src[c * RCV:(c + 1) * RCV, :]), reads=[MB], writes=[], sem_buf=TAB)
    TAB.lw = ("d", TAB.dsem, TAB.dcnt)


    LNSET = [(stats, mv, lnv, rstd)]
    dbg = {}
    if DEBUG:
        dbg["mb"] = dout("dbg_mb", [128, 3072]); dbg["u1b"] = dout("dbg_u1b", [128, DM], BF16); dbg["pu"] = dout("dbg_pu", [128, 4, 143])
        dbg["qT"] = dout("dbg_qT", [128, 4, 128], BF16); dbg["ymixT"] = dout("dbg_ymixT", [128, 8, 128], BF16); dbg["nd"] = dout("dbg_nd", [128, 4, 129])
        dbg["cols"] = dout("dbg_cols", [128, 16]); dbg["u1T"] = dout("dbg_u1T", [128, 8, 128], BF16)
        ST("sp", dbg["mb"], MB[:], MB)

    def ln_stats(src, sset=None):
        if sset is not None:
            stats, mv, lnv, rstd = sset
        else:
            stats, mv, lnv, rstd = LNSET[0]
        for j in range(2):
            T("dve", lambda e, j=j: e.bn_stats(out=stats[:, j, :], in_=src[:, j * 512:(j + 1) * 512]), [src], [stats])
        T("dve", lambda e: e.bn_aggr(out=mv[:], in_=stats[:].rearrange("p a b -> p (a b)")), [stats], [mv])
        T("act", lambda e: e.activation(out=lnv[:], in_=mv[:, 1:2], func=AF.Ln, bias=epsc[:], scale=1.0), [mv, epsc], [lnv])
        T("act", lambda e: e.activation(out=rstd[:], in_=lnv[:], func=AF.Exp, scale=-0.5), [lnv], [rstd])

    def ln_apply(src, dst, sset=None):
        if sset is not None:
            stats, mv, lnv, rstd = sset
        else:
            stats, mv, lnv, rstd = LNSET[0]
        T("dve", lambda e: e.tensor_scalar(out=dst[:], in0=src[:], scalar1=mv[:, 0:1], scalar2=rstd[:], op0=ALU.subtract, op1=ALU.mult), [src, mv, rstd], [dst])

    def ln_affine(src, tmp, dst, gbuf, g_ap, b_ap, sset=None):
        if sset is not None:
            stats, mv, lnv, rstd = sset
        else:
            stats, mv, lnv, rstd = LNSET[0]
        T("dve", lambda e: e.scalar_tensor_tensor(out=tmp[:], in0=src[:], scalar=mv[:, 0:1], in1=g_ap, op0=ALU.subtract, op1=ALU.mult), [src, mv, gbuf], [tmp])
        T("dve", lambda e: e.scalar_tensor_tensor(out=dst[:], in0=tmp[:], scalar=rstd[:], in1=b_ap, op0=ALU.mult, op1=ALU.add), [tmp, rstd, gbuf], [dst])

    def tile(i, pr):
        last_p = pr and (i == ntiles_prompt - 1)
        ti = i if pr else ntiles_prompt
        X = Xp[ti % 2]
        if ti == 0:
            LD("sp", X[:], xp[0:128, :], X)
        if pr and i + 1 < ntiles_prompt:
            LD("sp", Xp[(ti + 1) % 2][:], xp[(i + 1) * 128:(i + 2) * 128, :], Xp[(ti + 1) % 2])
        elif pr and do_sample:
            LD("sp", Xp[(ti + 1) % 2][:], xs, Xp[(ti + 1) % 2])
        ln_stats(X); ln_affine(X, T1, U1b, MB, MB[:, 1024:2048], MB[:, 0:1024])
        pb = ps(); pbv = pb.t[:].bitcast(BF16)
        def _tr(e, pbv=pbv):
            for k in range(8):
                ins = e.transpose(out=pbv[:, k * 128:(k + 1) * 128], in_=U1b[:, k * 128:(k + 1) * 128], identity=identb[:])
            return ins
        T("pe", _tr, [U1b, identb], [pb])
        T("act", lambda e, pbv=pbv: e.activation(out=U1T[:].rearrange("p k t -> p (k t)"), in_=pbv[:, 0:1024], func=AF.Copy), [pb], [U1T])
        if DEBUG and pr and i == 0:
            ST("sp", dbg["u1b"], U1b[:], U1b); ST("sp", dbg["u1T"], U1T[:], U1T)
        def fm_group(c0):
            pb = ps()
            def _mm(e, pb=pb):
                for j in range(4):
                    for k in range(8):
                        ins = e.matmul(pb[:, j * 128:(j + 1) * 128], lhsT=w_in_bf[:, k, c0 + j * 128:c0 + (j + 1) * 128], rhs=U1T[:, k, :], start=(k == 0), stop=(k == 7))
                return ins
            T("pe", _mm, [w_in_bf, U1T], [pb])
            return pb
        pb = fm_group(0)
        for j in range(4):
            if pr:
                o = PU[:, j, 15:143]; ii = pb[:, j * 128:(j + 1) * 128]; wbuf = PU
            else:
                o = PUS[:, j, :, 15:19]; ii = pb[:, j * 128:j * 128 + 64].rearrange("p (a b) -> p a b", b=4); wbuf = PUS
            T("act", lambda e, o=o, ii=ii, j=j: e.activation(out=o, in_=ii, func=AF.Identity, bias=bfm[:, j:j + 1], scale=1.0), [pb, bfm], [wbuf])
        pb = fm_group(512)
        for j in range(4):
            T("act", lambda e, pb=pb, j=j: e.activation(out=qT[:, j, :], in_=pb[:, j * 128:(j + 1) * 128], func=AF.Identity, bias=bfm[:, 4 + j:5 + j], scale=1.0), [pb, bfm], [qT])
        pb = fm_group(1024)
        for j in range(4):
            T("act", lambda e, pb=pb, j=j: e.activation(out=kT[:, j, :], in_=pb[:, j * 128:(j + 1) * 128], func=AF.Identity, bias=bks[:, j:j + 1], scale=KS_), [pb, bks], [kT])
        pb = fm_group(2048)
        for j in range(4):
            T("act", lambda e, pb=pb, j=j: e.activation(out=sgT[:, j, :], in_=pb[:, j * 128:(j + 1) * 128], func=AF.Sigmoid, bias=bfm[:, 16 + j:17 + j], scale=1.0), [pb, bfm], [sgT])
        pb = ps()
        def _mg(e, pb=pb):
            for g in range(2):
                for k in range(8):
                    ins = e.matmul(pb[0:4, g * 128:(g + 1) * 128], lhsT=w_in_bf[:, k, 2560 + 4 * g:2564 + 4 * g], rhs=U1T[:, k, :], start=(k == 0), stop=(k == 7))
            return ins
        T("pe", _mg, [w_in_bf, U1T], [pb])
        T("act", lambda e, pb=pb: e.activation(out=IG[:], in_=pb[0:4, 0:128], func=AF.Identity, bias=bgate[:, 0:1], scale=1.0), [pb, bgate], [IG])
        T("act", lambda e, pb=pb: e.activation(out=E1[:], in_=pb[0:4, 128:256], func=AF.Exp, bias=nbg[:], scale=-1.0), [pb, nbg], [E1])
        T("act", lambda e: e.activation(out=L1[:], in_=E1[:], func=AF.Ln, bias=onec[0:4, :], scale=1.0), [E1, onec], [L1])
        def tm_group(c0):
            pb = ps()
            def _mm(e, pb=pb):
                for k in range(8):
                    e.matmul(pb[:, 0:512], lhsT=U1T[:, k, :], rhs=w_in_bf[:, k, c0:c0 + 512], start=(k == 0), stop=False)
                return e.matmul(pb[:, 0:512], lhsT=ones_bf[0:1, :], rhs=brow_bf[0:1, c0:c0 + 512], start=False, stop=True)
            T("pe", _mm, [w_in_bf, U1T, ones_bf, brow_bf], [pb])
            return pb
        pb = tm_group(1024)
        T("act", lambda e, pb=pb: e.activation(out=ktok[:], in_=pb[:, 0:512], func=AF.Copy, scale=KS_), [pb], [ktok])
        pb = tm_group(1536)
        T("dve", lambda e, pb=pb: e.tensor_copy(vtok[:, :, 0:128], pb[:, 0:512].rearrange("p (h e) -> p h e", h=4)), [pb], [vtok])
        if last_p or not pr:
            pb = tm_group(0)
            T("act", lambda e, pb=pb: e.activation(out=PUT[:], in_=pb[:, 0:512], func=AF.Copy), [pb], [PUT])
            if pr:
                ST("sp", pool_p, PUT[113:128, :], PUT)
            else:
                for i4 in range(4):
                    ST("sp", pool_s[:, 11 + i4, :], PUT[i4:64:4, :], PUT)
        if pr:
            z = PU; A_, B_ = Wa, Wb
            def sl(buf, g0, g1, a, b):
                return buf[:, g0:g1, a:b]
            n0 = 15; n1 = 143
        else:
            z = PUS; A_, B_ = WaS, WbS
            def sl(buf, g0, g1, a, b):
                return buf[:, g0:g1, :, a:b]
            n0 = 15; n1 = 19
        T("dve", lambda e: e.tensor_tensor(out=sl(A_, 0, 4, 1, n1), in0=sl(z, 0, 4, 1, n1), in1=sl(z, 0, 4, 0, n1 - 1), op=ALU.add), [z], [A_])
        T("dve", lambda e: e.tensor_tensor(out=sl(B_, 1, 4, 3, n1), in0=sl(A_, 1, 4, 3, n1), in1=sl(A_, 1, 4, 1, n1 - 2), op=ALU.add), [A_], [B_])
        T("dve", lambda e: e.tensor_tensor(out=sl(A_, 2, 4, 7, n1), in0=sl(B_, 2, 4, 7, n1), in1=sl(B_, 2, 4, 3, n1 - 4), op=ALU.add), [B_], [A_])
        T("dve", lambda e: e.tensor_tensor(out=sl(B_, 3, 4, 15, n1), in0=sl(A_, 3, 4, 15, n1), in1=sl(A_, 3, 4, 7, n1 - 8), op=ALU.add), [A_], [B_])
        rc = rc0 if (pr and i == 0) else rcw
        srcs = [A_, B_, A_, B_]
        for g in range(4):
            if pr:
                o = Mm[:, g:g + 1, :]; r_ = rc[:, g:g + 1, :]
            else:
                o = Mm[:, g:g + 1, 0:64].rearrange("p g (a b) -> p g a b", b=4); r_ = rc[:, g:g + 1, 0:64].rearrange("p g (a b) -> p g a b", b=4)
            T("dve", lambda e, g=g, o=o, r_=r_: e.tensor_tensor(out=o, in0=sl(srcs[g], g, g + 1, n0, n1), in1=r_, op=ALU.mult), [srcs[g], rc], [Mm])
        if pr:
            T("dve", lambda e: e.tensor_tensor(out=dT[:], in0=Mm[:], in1=PU[:, :, 15:143], op=ALU.subtract), [Mm, PU], [dT])
            T("dve", lambda e: e.tensor_copy(PU[:, :, 0:15], PU[:, :, 128:143]), [PU], [PU])
        else:
            T("dve", lambda e: e.tensor_tensor(out=dT[:, :, 0:64].rearrange("p g (a b) -> p g a b", b=4), in0=Mm[:, :, 0:64].rearrange("p g (a b) -> p g a b", b=4), in1=PUS[:, :, :, 15:19], op=ALU.subtract), [Mm, PUS], [dT])
        pb = ps()
        def _mp(e, pb=pb):
            for g in range(4):
                ins = e.matmul(pb[:, g * 128:(g + 1) * 128], lhsT=w_pool_bf[:, g, :], rhs=dT[:, g, :], start=True, stop=True)
            return ins
        T("pe", _mp, [w_pool_bf, dT], [pb])
        for g in range(4):
            T("act", lambda e, pb=pb, g=g: e.activation(out=ymixT[:, g, :], in_=pb[:, g * 128:(g + 1) * 128], func=AF.Copy, scale=psc[:, g:g + 1]), [pb, psc], [ymixT])
        if pr:
            T("dve", lambda e: e.tensor_tensor_scan(out=Bt[:], data0=ones4[:], data1=L1[:], initial=Bprev[:, 0:1], op0=ALU.mult, op1=ALU.subtract), [ones4, L1, Bprev], [Bt])
            T("dve", lambda e: e.tensor_tensor(out=Gt[:], in0=IG[:], in1=Bt[:], op=ALU.subtract), [IG, Bt], [Gt])
            T("dve", lambda e: e.tensor_tensor_scan(out=Mgt[:], data0=Gt[:], data1=Gt[:], initial=Mprev[:, 0:1], op0=ALU.max, op1=ALU.max), [Gt, Mprev], [Mgt])
        else:
            v3 = lambda b_: b_[:].rearrange("p (a b) -> p a b", b=4)
            T("dve", lambda e: e.tensor_scalar(out=v3(Bt)[:, :, 0:1], in0=v3(L1)[:, :, 0:1], scalar1=-1.0, scalar2=None, op0=ALU.mult), [L1], [Bt])
            for ii in range(1, 4):
                T("dve", lambda e, ii=ii: e.tensor_tensor(out=v3(Bt)[:, :, ii:ii + 1], in0=v3(Bt)[:, :, ii - 1:ii], in1=v3(L1)[:, :, ii:ii + 1], op=ALU.subtract), [Bt, L1], [Bt])
            T("dve", lambda e: e.tensor_tensor(out=Gt[:], in0=IG[:], in1=Bt[:], op=ALU.subtract), [IG, Bt], [Gt])
            T("dve", lambda e: e.tensor_tensor(out=v3(Mgt)[:, :, 0:1], in0=v3(Gt)[:, :, 0:1], in1=m0T_sb[:].unsqueeze(2), op=ALU.max), [Gt, m0T_sb], [Mgt])
            for ii in range(1, 4):
                T("dve", lambda e, ii=ii: e.tensor_tensor(out=v3(Mgt)[:, :, ii:ii + 1], in0=v3(Mgt)[:, :, ii - 1:ii], in1=v3(Gt)[:, :, ii:ii + 1], op=ALU.max), [Mgt, Gt], [Mgt])
            T("dve", lambda e: e.tensor_copy(v3(MgE), v3(Mgt)[:, :, 3:4].broadcast_to([4, 32, 4])), [Mgt], [MgE])
        T("dve", lambda e: e.scalar_tensor_tensor(out=NMt[:], in0=Bt[:], scalar=-1.0, in1=Mgt[:], op0=ALU.mult, op1=ALU.subtract), [Bt, Mgt], [NMt])
        if pr:
            T("dve", lambda e: e.tensor_copy(Bprev[:], Bt[:, 127:128]), [Bt], [Bprev])
            T("dve", lambda e: e.tensor_copy(Mprev[:], Mgt[:, 127:128]), [Mgt], [Mprev])
        pb = ps()
        def _trg(e, pb=pb):
            srcl = [Gt, Mgt, NMt] + ([] if pr else [MgE])
            for n_, s_ in enumerate(srcl):
                ins = e.transpose(out=pb[:, 4 * n_:4 * n_ + 4], in_=s_[:], identity=ident[0:4, 0:4])
            return ins
        T("pe", _trg, [Gt, Mgt, NMt, MgE, ident], [pb])
        ncol = 12 if pr else 16
        T("act", lambda e, pb=pb: e.activation(out=COLS[:, 0:ncol], in_=pb[:, 0:ncol], func=AF.Copy), [pb], [COLS])
        pb = ps()
        def _mb(e, pb=pb):
            for h in range(4):
                ins = e.matmul(pb[:, h * 128:(h + 1) * 128], lhsT=selh[0:4, h * 128:(h + 1) * 128], rhs=Mgt[:], start=True, stop=True)
            return ins
        T("pe", _mb, [selh, Mgt], [pb])
        T("dve", lambda e, pb=pb: e.tensor_copy(MgB[:].rearrange("p h t -> p (h t)"), pb[:, 0:512]), [pb], [MgB])
        if pr:
            prevc = MgprevB[:]; endc = MgB[:, :, 127]; ginref = MgB[:, :, 127]; prevbuf = MgprevB
        else:
            prevc = m0col_sb[:]; ginref = COLS[:, 12:16]; prevbuf = m0col_sb
        T("dve", lambda e: e.tensor_tensor(out=tmp4[:], in0=prevc, in1=COLS[:, 4:8], op=ALU.subtract), [prevbuf, COLS], [tmp4])
        T("dve", lambda e: e.tensor_tensor(out=tmp4b[:], in0=COLS[:, 0:4], in1=ginref, op=ALU.subtract), [COLS, MgB], [tmp4b])
        if pr:
            T("dve", lambda e: e.tensor_tensor(out=tmp4c[:], in0=MgprevB[:], in1=MgB[:, :, 127], op=ALU.subtract), [MgprevB, MgB], [tmp4c])
        T("act", lambda e: e.activation(out=EMT[:], in_=COLS[:, 8:12], func=AF.Exp), [COLS], [EMT])
        T("act", lambda e: e.activation(out=WP[:], in_=tmp4[:], func=AF.Exp), [tmp4], [WP])
        T("act", lambda e: e.activation(out=GIN[:], in_=tmp4b[:], func=AF.Exp), [tmp4b], [GIN])
        if pr:
            T("act", lambda e: e.activation(out=GP[:], in_=tmp4c[:], func=AF.Exp), [tmp4c], [GP])
        else:
            pbm = ps()
            def _m0(e, pbm=pbm):
                for h in range(4):
                    ins = e.matmul(pbm[:, h * 32:(h + 1) * 32], lhsT=selh[0:4, h * 128:(h + 1) * 128], rhs=m0T_sb[:], start=True, stop=True)
                return ins
            T("pe", _m0, [selh, m0T_sb], [pbm])
            T("dve", lambda e, pbm=pbm: e.tensor_copy(m0B[:].rearrange("p h j -> p (h j)"), pbm[:, 0:128]), [pbm], [m0B])
            T("dve", lambda e: e.tensor_tensor(out=GPB[:], in0=m0B[:, :, 0:16], in1=MgB[:].rearrange("p h (j i) -> p h j i", i=4)[:, :, 0:16, 3], op=ALU.subtract), [m0B, MgB], [GPB])
            T("act", lambda e: e.activation(out=GPB[:], in_=GPB[:], func=AF.Exp), [GPB], [GPB])
        mask = maskp if pr else masks
        for h in range(4):
            a_ = Aw[h % 2]; e_ = Ew[h % 2]; pt_ = PT[h % 2]; tn_ = TMPN[h % 2]
            pbS = ps()
            T("pe", lambda e, pbS=pbS, h=h: e.matmul(pbS[:, 0:128], lhsT=kT[:, h, :], rhs=qT[:, h, :], start=True, stop=True), [kT, qT], [pbS])
            T("dve", lambda e, a_=a_, h=h: e.scalar_tensor_tensor(out=a_[:], in0=MgB[:, h, :], scalar=COLS[:, h:h + 1], in1=mask[:], op0=ALU.subtract, op1=ALU.add), [MgB, COLS, mask], [a_])
            T("act", lambda e, a_=a_, e_=e_: e.activation(out=e_[:], in_=a_[:], func=AF.Exp, scale=-1.0), [a_], [e_])
            T("dve", lambda e, pbS=pbS, e_=e_, pt_=pt_: e.tensor_tensor(out=pt_[:], in0=pbS[:, 0:128], in1=e_[:], op=ALU.mult), [pbS, e_], [pt_])
            pbI = ps()
            T("pe", lambda e, pbI=pbI, pt_=pt_, h=h: e.matmul(pbI[:, 0:129], lhsT=pt_[:], rhs=vtok[:, h, :], start=True, stop=True), [pt_, vtok], [pbI])
            pbN = ps()
            if pr:
                T("pe", lambda e, pbN=pbN, h=h: e.matmul(pbN[:, 0:129], lhsT=qT[:, h, :], rhs=Sbf[:, h, :], start=True, stop=True), [qT, Sbf], [pbN])
            else:
                S0h = S0hp[h % 2]
                if h == 0:
                    LD("act", n0all[:], n0fm, n0all)
                    P.dma("sp", lambda e: e.dma_start(out=S0hp[0][:, :, 0:128], in_=S0d[:, 0].rearrange("j d e -> d j e")), reads=[], writes=[S0hp[0]])
                if h + 1 < 4:
                    P.dma("sp", lambda e, h=h: e.dma_start(out=S0hp[(h + 1) % 2][:, :, 0:128], in_=S0d[:, h + 1].rearrange("j d e -> d j e")), reads=[], writes=[S0hp[(h + 1) % 2]])
                T("act", lambda e, h=h, S0h=S0h: e.activation(out=S0h[:, :, 128], in_=n0all[:, h, :], func=AF.Copy), [n0all], [S0h])
                T("act", lambda e, S0h=S0h: e.activation(out=S0bf[:], in_=S0h[:], func=AF.Copy), [S0h], [S0bf])
                T("dve", lambda e, h=h: e.tensor_copy(qTz[:, 0:2112].rearrange("p (j r) -> p j r", r=132)[:, :, 0:4], qT[:, h, 0:64].rearrange("p (j i) -> p j i", i=4)), [qT], [qTz])
                def _mi(e, pbN=pbN):
                    for j in range(16):
                        ins = e.matmul(pbN[:, 0:129], lhsT=qTz[:, j * 128:(j + 1) * 128], rhs=S0bf[:, j, :], start=(j == 0), stop=(j == 15))
                    return ins
                T("pe", _mi, [qTz, S0bf], [pbN])
            T("act", lambda e, pbN=pbN, tn_=tn_, h=h: e.activation(out=tn_[:], in_=pbN[:, 0:129], func=AF.Copy, scale=WP[:, h:h + 1]), [pbN, WP], [tn_])
            T("dve", lambda e, pbI=pbI, tn_=tn_, h=h: e.tensor_tensor(out=ND[:, h, :], in0=pbI[:, 0:129], in1=tn_[:], op=ALU.add), [pbI, tn_], [ND])
            if pr:
                kg = KG[h % 2]
                T("dve", lambda e, kg=kg, h=h: e.tensor_scalar(out=kg[:], in0=ktok[:, h * 128:(h + 1) * 128], scalar1=GIN[:, h:h + 1], scalar2=None, op0=ALU.mult), [ktok, GIN], [kg])
                pbU = ps()
                T("pe", lambda e, pbU=pbU, kg=kg, h=h: e.matmul(pbU[:, 0:129], lhsT=kg[:], rhs=vtok[:, h, :], start=True, stop=True), [kg, vtok], [pbU])
                T("dve", lambda e, pbU=pbU, h=h: e.scalar_tensor_tensor(out=S32[:, h, :], in0=S32[:, h, :], scalar=GP[:, h:h + 1], in1=pbU[:, 0:129], op0=ALU.mult, op1=ALU.add), [S32, GP, pbU], [S32])
            else:
                T("dve", lambda e, h=h: e.tensor_scalar(out=GINM[:], in0=bmask[:], scalar1=GIN[:, h:h + 1], scalar2=None, op0=ALU.mult), [bmask, GIN], [GINM])
                T("dve", lambda e, h=h: e.tensor_tensor(out=KGZ[:], in0=ktok[:, h * 128:(h + 1) * 128].unsqueeze(1).broadcast_to([128, 16, 128]), in1=GINM[:].unsqueeze(2).broadcast_to([128, 16, 128]), op=ALU.mult), [ktok, GINM], [KGZ])
                for j in range(16):
                    pbU = ps()
                    T("pe", lambda e, pbU=pbU, j=j, h=h: e.matmul(pbU[:, 0:129], lhsT=KGZ[:, j, :], rhs=vtok[:, h, :], start=True, stop=True), [KGZ, vtok], [pbU])
                    T("dve", lambda e, pbU=pbU, j=j, h=h: e.scalar_tensor_tensor(out=SN[:, j, :], in0=S0hp[h % 2][:, j, :], scalar=GPB[:, h, j:j + 1], in1=pbU[:, 0:129], op0=ALU.mult, op1=ALU.add), [S0hp[h % 2], GPB, pbU], [SN])
                ST("sp", C_s[:, h].rearrange("j d e -> d j e"), SN[:, :, 0:128], SN)
                pbn = ps()
                T("pe", lambda e, pbn=pbn: e.transpose(out=pbn[0:16, 0:128], in_=SN[:, :, 128], identity=ident[:]), [SN, ident], [pbn])
                T("act", lambda e, pbn=pbn: e.activation(out=NS[:], in_=pbn[0:16, 0:128], func=AF.Copy), [pbn], [NS])
                ST("sp", n_s[:, h, :], NS[:], NS)
        if pr:
            T("act", lambda e: e.activation(out=Sbf[:], in_=S32[:], func=AF.Copy), [S32], [Sbf])
            T("dve", lambda e: e.tensor_copy(MgprevB[:], MgB[:, :, 127]), [MgB], [MgprevB])
            if last_p:
                ST("sp", C_p.rearrange("h d e -> d h e"), S32[:, :, 0:128], S32)
                pbn = ps()
                T("pe", lambda e, pbn=pbn: e.transpose(out=pbn[0:4, 0:128], in_=S32[:, :, 128], identity=ident[:]), [S32, ident], [pbn])
                NPo = sb([4, 128], F32, "NPo"); MPo = sb([4, 1], F32, "MPo")
                T("act", lambda e, pbn=pbn: e.activation(out=NPo[:], in_=pbn[0:4, 0:128], func=AF.Copy), [pbn], [NPo])
                ST("sp", n_p, NPo[:], NPo)
                T("dve", lambda e: e.tensor_scalar(out=MPo[:], in0=NMt[:, 127:128], scalar1=-1.0, scalar2=None, op0=ALU.mult), [NMt], [MPo])
                ST("sp", m_p, MPo[:], MPo)
        else:
            T("dve", lambda e: e.tensor_scalar(out=MSo[:], in0=NMt[:].rearrange("p (j i) -> p j i", i=4)[:, 0:16, 3], scalar1=-1.0, scalar2=None, op0=ALU.mult), [NMt], [MSo])
            ST("sp", m_s, MSo[:], MSo)
        T("dve", lambda e: e.scalar_tensor_tensor(out=absd[:], in0=ND[:, :, 128], scalar=-1.0, in1=ND[:, :, 128], op0=ALU.mult, op1=ALU.max), [ND], [absd])
        T("dve", lambda e: e.tensor_tensor(out=absd[:], in0=absd[:], in1=EMT[:], op=ALU.max), [absd, EMT], [absd])
        T("dve", lambda e: e.reciprocal(out=RD[:], in_=absd[:]), [absd], [RD])
        for h in range(4):
            T("dve", lambda e, h=h: e.tensor_scalar(out=HH[:, h, :], in0=ND[:, h, 0:128], scalar1=RD[:, h:h + 1], scalar2=None, op0=ALU.mult), [ND, RD], [HH])
            T("dve", lambda e, h=h: e.bn_stats(out=st4[:, h, :], in_=HH[:, h, :]), [HH], [st4])
            T("dve", lambda e, h=h: e.bn_aggr(out=MV4[:, h, :], in_=st4[:, h, :]), [st4], [MV4])
        T("act", lambda e: e.activation(out=LN4[:], in_=MV4[:, :, 1], func=AF.Ln, bias=epsc[:], scale=1.0), [MV4, epsc], [LN4])
        T("act", lambda e: e.activation(out=RS4[:], in_=LN4[:], func=AF.Exp, scale=-0.5), [LN4], [RS4])
        pb = ps(); pbv = pb.t[:].bitcast(BF16)
        for h in range(4):
            T("dve", lambda e, h=h: e.tensor_scalar(out=HN[:, h, :], in0=HH[:, h, :], scalar1=MV4[:, h, 0:1], scalar2=RS4[:, h:h + 1], op0=ALU.subtract, op1=ALU.mult), [HH, MV4, RS4], [HN])
        def _trh(e, pbv=pbv):
            for h in range(4):
                ins = e.transpose(out=pbv[:, h * 128:(h + 1) * 128], in_=HN[:, h, :], identity=identb[:])
            return ins
        T("pe", _trh, [HN, identb], [pb])
        for h in range(4):
            T("dve", lambda e, pbv=pbv, h=h: e.scalar_tensor_tensor(out=ymixT[:, 4 + h, :], in0=pbv[:, h * 128:(h + 1) * 128], scalar=mhgs[:, h:h + 1], in1=sgT[:, h, :], op0=ALU.mult, op1=ALU.mult), [pb, mhgs, sgT], [ymixT])
        if DEBUG and pr and i == 0:
            ST("sp", dbg["pu"], PU[:], PU); ST("sp", dbg["qT"], qT[:], qT); ST("sp", dbg["ymixT"], ymixT[:], ymixT); ST("sp", dbg["nd"], ND[:], ND); ST("sp", dbg["cols"], COLS[:], COLS)
        for hf in range(2):
            pb = ps()
            def _mo(e, pb=pb, hf=hf):
                for k in range(8):
                    ins = e.matmul(pb[:, 0:512], lhsT=ymixT[:, k, :], rhs=w_out_bf[:, k, hf * 512:(hf + 1) * 512], start=(k == 0), stop=(k == 7))
                return ins
            T("pe", _mo, [ymixT, w_out_bf], [pb])
            T("dve", lambda e, pb=pb, hf=hf: e.tensor_tensor(out=T2[:, hf * 512:(hf + 1) * 512], in0=pb[:, 0:512], in1=MB[:, 2048 + hf * 512:2048 + (hf + 1) * 512], op=ALU.mult), [pb, MB], [T2])
        T("dve", lambda e: e.scalar_tensor_tensor(out=T2[:], in0=X[:], scalar=ALPHA_, in1=T2[:], op0=ALU.mult, op1=ALU.add), [X, T2], [T2])
        ln_stats(T2); ln_affine(T2, T1, X1, LNG, LNG[:, 0, :], LNG[:, 1, :])
        P.dma("sp", lambda e: e.dma_start(out=x1_dram[ti * 128:(ti + 1) * 128, :], in_=X1[:]), reads=[X1], writes=[x1_dram], sem_buf=x1_dram)
        if not do_peer:
            if pr:
                ST("sp", yp[i * 128:(i + 1) * 128, :], X1[:], X1)
            else:
                ST("sp", ys, X1[0:64, :], X1)

    def stageR(i, pr, pp, p3=0, preloaded=False):
        ti = i if pr else ntiles_prompt
        X1 = X1p[p3]; U2 = U2p[pp]; EIDX = EIDXp[pp]; GG = GGp[pp]; MBB = MBBp if pr else MBBs
        T1 = T1r
        if not preloaded:
            LD("sp", X1[:], x1_dram[ti * 128:(ti + 1) * 128, :], X1, r=[x1_dram])
        ln_stats(X1, LNR); ln_affine(X1, T1, U2, MBB, MBB[:, 1024:2048], MBB[:, 0:1024], LNR)
        for half in range(2):
            pb = ps()
            def _tr2(e, pb=pb, half=half):
                for k in range(4):
                    kk = half * 4 + k
                    ins = e.transpose(out=pb[:, k * 128:(k + 1) * 128], in_=U2[:, kk * 128:(kk + 1) * 128], identity=ident[:])
                return ins
            T("pe", _tr2, [U2, ident], [pb])
            T("act", lambda e, pb=pb, half=half: e.activation(out=U2T[:, half * 4:(half + 1) * 4, :].rearrange("p k t -> p (k t)"), in_=pb[:, 0:512], func=AF.Copy), [pb], [U2T])
        pb2s = {}
        def k1_load(hp):
            wq = wqst[hp % 3]
            LD("sp", wq[:], w_q[hp], wq)
        def k1_front(hp):
            wq = wqst[hp % 3]; qs = qTs[hp % 2]
            pb = ps()
            def _mq(e, pb=pb, wq=wq):
                for k in range(8):
                    ins = e.matmul(pb[:, 0:128], lhsT=wq[:, k, :], rhs=U2T[:, k, :], start=(k == 0), stop=(k == 7))
                return ins
            T("pe", _mq, [wq, U2T], [pb])
            T("act", lambda e, pb=pb, qs=qs: e.activation(out=qs[:], in_=pb[:, 0:128], func=AF.Copy), [pb], [qs])
            pb2 = ps()
            T("pe", lambda e, pb2=pb2, qs=qs, hp=hp: e.matmul(pb2[:, 0:128], lhsT=qs[:], rhs=keys_sb[:, hp, :], start=True, stop=True), [qs, keys_sb], [pb2])
            pb2s[hp] = pb2
        def k1_back(hp):
            pb2 = pb2s[hp]; s2 = SC2[hp % 2]
            T("dve", lambda e, pb2=pb2, hp=hp: e.max(out=sv[:, hp, 0:8], in_=pb2[:, 0:128]), [pb2], [sv])
            T("dve", lambda e, pb2=pb2, hp=hp: e.max_index(out=si[:, hp, 0:8], in_max=sv[:, hp, 0:8], in_values=pb2[:, 0:128]), [pb2, sv], [si])
            T("dve", lambda e, pb2=pb2, hp=hp, s2=s2: e.match_replace(out=s2[:], in_to_replace=sv[:, hp, 0:8], in_values=pb2[:, 0:128], imm_value=-1e30), [pb2, sv], [s2])
            T("dve", lambda e, hp=hp, s2=s2: e.max(out=sv[:, hp, 8:16], in_=s2[:]), [s2], [sv])
            T("dve", lambda e, hp=hp, s2=s2: e.max_index(out=si[:, hp, 8:16], in_max=sv[:, hp, 8:16], in_values=s2[:]), [s2, sv], [si])
        k1_load(0); k1_load(1); k1_front(0)
        for hp in range(16):
            if hp + 2 < 16:
                k1_load(hp + 2)
            if hp + 1 < 16:
                k1_front(hp + 1)
            k1_back(hp)
        T("dve", lambda e: e.tensor_copy(sif[:], si[:]), [si], [sif])
        for h in range(8):
            cbuf = comb[h % 2]; c2 = comb2[h % 2]
            T("dve", lambda e, h=h, cbuf=cbuf: e.tensor_tensor(out=cbuf[:], in0=sv[:, 2 * h, :].unsqueeze(2).broadcast_to([128, 16, 16]), in1=sv[:, 2 * h + 1, :].unsqueeze(1).broadcast_to([128, 16, 16]), op=ALU.add), [sv], [cbuf])
            cf = cbuf[:].rearrange("p a b -> p (a b)")
            T("dve", lambda e, h=h, cf=cf: e.max(out=c8[:, h, 0:8], in_=cf), [cbuf], [c8])
            T("dve", lambda e, h=h, cf=cf: e.max_index(out=cpos[:, h, 0:8], in_max=c8[:, h, 0:8], in_values=cf), [cbuf, c8], [cpos])
            T("dve", lambda e, h=h, cf=cf, c2=c2: e.match_replace(out=c2[:], in_to_replace=c8[:, h, 0:8], in_values=cf, imm_value=-1e30), [cbuf, c8], [c2])
            T("dve", lambda e, h=h, c2=c2: e.max(out=c8[:, h, 8:16], in_=c2[:]), [c2], [c8])
            T("dve", lambda e, h=h, c2=c2: e.max_index(out=cpos[:, h, 8:16], in_max=c8[:, h, 8:16], in_values=c2[:]), [c2, c8], [cpos])
        cpf = cpos[:].rearrange("p h k -> p (h k)")
        T("dve", lambda e: e.tensor_single_scalar(out=ca[:], in_=cpf, scalar=4, op=ALU.logical_shift_right), [cpos], [ca])
        T("dve", lambda e: e.tensor_single_scalar(out=cb[:], in_=cpf, scalar=15, op=ALU.bitwise_and), [cpos], [cb])
        T("dve", lambda e: e.tensor_copy(caf[:], ca[:]), [ca], [caf])
        T("dve", lambda e: e.tensor_copy(cbf[:], cb[:]), [cb], [cbf])
        iob = iota16b[:].unsqueeze(1).broadcast_to([128, 128, 16])
        for (cf_, pidx, dst) in ((caf, 0, i1s), (cbf, 1, i2s)):
            T("dve", lambda e, cf_=cf_: e.tensor_tensor(out=oh[:], in0=cf_[:].unsqueeze(2).broadcast_to([128, 128, 16]), in1=iob, op=ALU.is_equal), [cf_, iota16b], [oh])
            sview = sif[:].rearrange("p (h q) a -> p h q a", q=2)[:, :, pidx, :]
            T("dve", lambda e, sview=sview: e.tensor_tensor(out=oh2[:].rearrange("p (h k) a -> p h k a", h=8), in0=oh[:].rearrange("p (h k) a -> p h k a", h=8), in1=sview.unsqueeze(2).broadcast_to([128, 8, 16, 16]), op=ALU.mult), [oh, sif], [oh2])
            T("dve", lambda e, dst=dst: e.tensor_reduce(out=dst[:], in_=oh2[:], axis=AX.X, op=ALU.add), [oh2], [dst])
        T("dve", lambda e: e.scalar_tensor_tensor(out=i1s[:], in0=i1s[:], scalar=128.0, in1=i2s[:], op0=ALU.mult, op1=ALU.add), [i1s, i2s], [i1s])
        T("dve", lambda e: e.tensor_copy(EIDX[:], i1s[:]), [i1s], [EIDX])
        if not pr:
            T("dve", lambda e: e.memset(EIDX[64:128, :], 1 << 30), [], [EIDX])
        T("dve", lambda e: e.tensor_tensor(out=cm[:], in0=c8[:], in1=c8[:, :, 0:1].broadcast_to([128, 8, 16]), op=ALU.subtract), [c8], [cm])
        T("act", lambda e: e.activation(out=ce[:], in_=cm[:], func=AF.Exp), [cm], [ce])
        T("dve", lambda e: e.tensor_reduce(out=csum[:], in_=ce[:], axis=AX.X, op=ALU.add), [ce], [csum])
        T("dve", lambda e: e.reciprocal(out=csum[:], in_=csum[:]), [csum], [csum])
        T("dve", lambda e: e.tensor_tensor(out=GG[:], in0=ce[:], in1=csum[:].unsqueeze(2).broadcast_to([128, 8, 16]), op=ALU.mult), [ce, csum], [GG])
    def stageG(i, pr, pp, dl, p3=0):
        X1 = X1p[p3]; U2 = U2p[pp]; EIDX = EIDXp[pp]; GG = GGp[pp]; MBB = MBBp if pr else MBBs
        nper = (len(dl) + 111) // 112
        GGf = GG[:].rearrange("p h k -> p (h k)")
        def dbuild(sl_):
            d_ = Dg[sl_ % 4]; pz = sl_ % 2; cs = sl_ // 2
            b_ = rb[sl_ % NRB]
            cf = CFp[pz]
            T("act", lambda e, cf=cf, pz=pz, cs=cs, sl_=sl_: e.activation(out=cf[:, cs:cs + 1], in_=GELp[pz][:, cs:cs + 1], func=AF.Copy, scale=GGf[:, sl_:sl_ + 1]), [GELp[pz], GG], [cf])
            T("act", lambda e, d_=d_, cf=cf, cs=cs: e.activation(out=d_[:], in_=identb[:], func=AF.Copy, scale=cf[:, cs:cs + 1]), [identb, cf], [d_])
            def _mv(e, b_=b_, d_=d_, sl_=sl_):
                e.matmul(PSY[0][:, 0:512], lhsT=d_[:], rhs=b_[:, 1024:1536], start=(sl_ == 0), stop=(sl_ == NSLOT - 1))
                return e.matmul(PSY[1][:, 0:512], lhsT=d_[:], rhs=b_[:, 1536:2048], start=(sl_ == 0), stop=(sl_ == NSLOT - 1))
            T("pe", _mv, [b_, d_], [PSY[0], PSY[1]])
        for sl_ in range(NSLOT):
            b_ = rb[sl_ % NRB]; pz = sl_ % 2; cs = sl_ // 2
            if pr:
                P.dma("pool", lambda e, b_=b_, sl_=sl_: e.indirect_dma_start(out=b_[:], out_offset=None, in_=TAB[:, :], in_offset=bass.IndirectOffsetOnAxis(ap=EIDX[:, sl_:sl_ + 1], axis=0)), reads=[EIDX, TAB], writes=[b_])
            else:
                def _gs(e, b_=b_, sl_=sl_):
                    if sl_ == 0:
                        e.reg_mov(bcreg, 16383)
                    return e.indirect_dma_start(out=b_[:], out_offset=None, in_=TAB[:, :], in_offset=bass.IndirectOffsetOnAxis(ap=EIDX[:, sl_:sl_ + 1], axis=0), bounds_check=bcreg, oob_is_err=False)
                P.dma("pool", _gs, reads=[EIDX, TAB], writes=[b_])
            T("dve", lambda e, b_=b_, pz=pz, cs=cs, sl_=sl_: e.scalar_tensor_tensor(out=junkp[sl_ % 4][:], in0=b_[:, 0:1024], scalar=1.0, in1=U2[:], op0=ALU.mult, op1=ALU.mult, accum_out=ACTVp[pz][:, cs:cs + 1]), [b_, U2], [junkp[sl_ % 4], ACTVp[pz]])
            T("act", lambda e, pz=pz, cs=cs: e.activation(out=GELp[pz][:, cs:cs + 1], in_=ACTVp[pz][:, cs:cs + 1], func=AF.Gelu), [ACTVp[pz]], [GELp[pz]])
            if sl_ >= 1:
                dbuild(sl_ - 1)
            for _ in range(nper):
                if dl:
                    dl.pop(0)()
        dbuild(NSLOT - 1)
        while dl:
            dl.pop(0)()
        for hf in range(2):
            T("dve", lambda e, hf=hf: e.tensor_tensor(out=T2[:, hf * 512:(hf + 1) * 512], in0=PSY[hf][:, 0:512], in1=MBB[:, 2048 + hf * 512:2048 + (hf + 1) * 512], op=ALU.mult), [PSY[hf], MBB], [T2])
        T("dve", lambda e: e.scalar_tensor_tensor(out=T2[:], in0=X1[:], scalar=ALPHA_, in1=T2[:], op0=ALU.mult, op1=ALU.add), [X1, T2], [T2])
        T1 = T1g
        ln_stats(T2); ln_affine(T2, T1, OUT, LNG2, LNG2[:, 0, :], LNG2[:, 1, :])
        if pr:
            ST("sp", yp[i * 128:(i + 1) * 128, :], OUT[:], OUT)
        else:
            ST("sp", ys, OUT[0:64, :], OUT)

    for i in range(ntiles_prompt):
        tile(i, True)
    if do_sample:
        T("dve", lambda e: e.memset(dT[:], 0.0), [], [dT])
        for half in range(2):
            for ii in range(4):
                LD("sp", MB[half * 64 + ii:half * 64 + 64:4, :], mod_dram[1:17, 0:3072], MB, r=[mod_dram])
        LD("sp", PUS[:, :, :, 0:15], spool, PUS)
        dummy = sb([1, 1], F32, "dummyb")
        P.dma("act", lambda e: e.dma_start(out=pool_s[:, 0:11, :], in_=spool_raw[:, 4:15, :]), reads=[], writes=[dummy], sem_buf=dummy, is_output=True)
        tile(0, False)
    P.end_phase()
    if do_peer:
        P.begin_phase()
        ident = sb([128, 128], F32, "identB"); identb = sb([128, 128], BF16, "identbB"); iota16 = sb([128, 16], F32, "iota16B")
        LD("sp", ident[:], c_ident, ident); LD("sp", iota16[:], c_iota, iota16)
        T("act", lambda e: e.activation(out=identb[:], in_=ident[:], func=AF.Copy), [ident], [identb])
        epsc = sb([128, 1], F32, "epscB")
        T("dve", lambda e: e.memset(epsc[:], EPS_), [], [epsc])
        stats = sb([128, 2, 6], F32, "statsB"); mv = sb([128, 2], F32, "mvB"); lnv = sb([128, 1], F32, "lnvB"); rstd = sb([128, 1], F32, "rstdB")
        LNG2 = sb([128, 2, DM], F32, "LNG2")
        for j, a in enumerate((ln2g, ln2b)):
            LD("act", LNG2[:, j, :], a.broadcast_to([128, DM]), LNG2)
        keys_sb = sb([128, 16, 128], F32, "keys_sb")
        LD("act", keys_sb[:], keysT, keys_sb)
        MBBp = sb([128, 3072], F32, "MBBp"); MBBs = sb([128, 3072], F32, "MBBs")
        LD("sp", MBBp[:], mod_dram[0:1, 3072:6144].broadcast_to([128, 3072]), MBBp, r=[mod_dram])
        for half in range(2):
            for ii in range(4):
                LD("act", MBBs[half * 64 + ii:half * 64 + 64:4, :], mod_dram[1:17, 3072:6144], MBBs, r=[mod_dram])
        X1p = [sb([128, DM], F32, f"X1B{i}") for i in range(3)]; T1r = sb([128, DM], F32, "T1r"); T1g = sb([128, DM], F32, "T1g"); T2 = sb([128, DM], F32, "T2B")
        U2p = [sb([128, DM], F32, f"U2{i}") for i in range(2)]; U2T = sb([128, 8, 128], F32, "U2T"); OUT = sb([128, DM], F32, "OUT")
        LNR = (sb([128, 2, 6], F32, "statsR"), sb([128, 2], F32, "mvR"), sb([128, 1], F32, "lnvR"), sb([128, 1], F32, "rstdR"))
        LNSET[0] = (stats, mv, lnv, rstd)
        CFp = [sb([128, 64], F32, f"CF{i}") for i in range(2)]
        wqst = [sb([128, 8, 128], F32, f"wqst{i}") for i in range(3)]
        qTs = [sb([128, 128], F32, f"qTs{i}") for i in range(2)]
        SC2 = [sb([128, 128], F32, f"SC2{i}") for i in range(2)]
        sv = sb([128, 16, 16], F32, "sv"); si = sb([128, 16, 16], U32, "si"); sif = sb([128, 16, 16], BF16, "sif")
        comb = [sb([128, 16, 16], F32, f"comb{i}") for i in range(2)]; comb2 = [sb([128, 256], F32, f"comb2{i}") for i in range(2)]
        c8 = sb([128, 8, 16], F32, "c8"); cpos = sb([128, 8, 16], U32, "cpos"); ca = sb([128, 128], U32, "ca"); cb = sb([128, 128], U32, "cb")
        caf = sb([128, 128], BF16, "caf"); cbf = sb([128, 128], BF16, "cbf")
        oh = sb([128, 128, 16], BF16, "oh"); oh2 = sb([128, 128, 16], BF16, "oh2"); iota16b = sb([128, 16], BF16, "iota16b")
        T("act", lambda e: e.activation(out=iota16b[:], in_=iota16[:], func=AF.Copy), [iota16], [iota16b])
        i1s = sb([128, 128], F32, "i1s"); i2s = sb([128, 128], F32, "i2s"); EIDXp = [sb([128, 128], I32, f"EIDX{i}") for i in range(2)]
        cm = sb([128, 8, 16], F32, "cm"); ce = sb([128, 8, 16], F32, "ce"); csum = sb([128, 8], F32, "csum"); GGp = [sb([128, 8, 16], F32, f"GG{i}") for i in range(2)]
        ACTVp = [sb([128, 64], F32, f"ACTV{i}") for i in range(2)]; GELp = [sb([128, 64], F32, f"GEL{i}") for i in range(2)]
        NRB = 20
        bcreg = nc.gpsimd.alloc_register("bcreg")
        rb = [sb([128, 2048], BF16, f"rb{i}") for i in range(NRB)]
        junkp = [sb([128, DM], BF16, f"junk{i}") for i in range(4)]
        Dg = [sb([128, 128], BF16, f"Dg{i}") for i in range(4)]

        tl = [(i, True) for i in range(ntiles_prompt)] + ([(0, False)] if do_sample else [])
        def tix(n_):
            return tl[n_][0] if tl[n_][1] else ntiles_prompt
        stageR(tl[0][0], tl[0][1], 0, 0)
        if len(tl) > 1:
            LD("sp", X1p[1][:], x1_dram[tix(1) * 128:(tix(1) + 1) * 128, :], X1p[1], r=[x1_dram])
        for n_, (i, pr) in enumerate(tl):
            dl = []
            if n_ + 2 < len(tl):
                LD("sp", X1p[(n_ + 2) % 3][:], x1_dram[tix(n_ + 2) * 128:(tix(n_ + 2) + 1) * 128, :], X1p[(n_ + 2) % 3], r=[x1_dram])
            if n_ + 1 < len(tl):
                defer[0] = dl
                stageR(tl[n_ + 1][0], tl[n_ + 1][1], (n_ + 1) % 2, (n_ + 1) % 3, preloaded=True)
                defer[0] = None
            stageG(i, pr, n_ % 2, dl, n_ % 3)
    P.finish()
    return nc


def _consts():
    c = {}
    c["c_ident"] = np.eye(128, dtype=np.float32)
    s = np.arange(128)[:, None]; t = np.arange(128)[None, :]
    c["c_maskp"] = np.where(s <= t, 0.0, 1e4).astype(np.float32)
    c["c_masks"] = np.where((s <= t) & (s // 4 == t // 4), 0.0, 1e4).astype(np.float32)
    bm = np.zeros((128, 16), np.float32)
    for tt in range(64):
        bm[tt, tt // 4] = 1.0
    c["c_bmask"] = bm
    sel = np.zeros((4, 512), np.float32)
    for h in range(4):
        sel[h, h * 128:(h + 1) * 128] = 1.0
    c["c_selh"] = sel
    c["c_iota"] = np.tile(np.arange(16, dtype=np.float32)[None, :], (128, 1))
    rc0 = np.zeros((128, 4, 128), np.float32); rcw = np.zeros((128, 4, 128), np.float32)
    for g, w in enumerate((2, 4, 8, 16)):
        rc0[:, g, :] = 1.0 / np.minimum(np.arange(128) + 1, w)
        rcw[:, g, :] = 1.0 / w
    c["c_rc0"] = rc0; c["c_rcw"] = rcw
    return c


def kernel(**inp):
    f = lambda a: np.ascontiguousarray(np.asarray(a), dtype=np.float32)
    g = {k: f(v) for k, v in inp.items()}
    b_in = g["b_in"][0]
    shared = {
        "w_mod": g["w_mod"][0], "b_mod": g["b_mod"][0][None, :], "w_in": g["w_in"][0],
        "b_fm": f(b_in[:2560].reshape(20, 128).T), "b_gate": f(np.stack([b_in[2560:2564], b_in[2564:2568]], axis=1)),
        "b_row": b_in[None, :], "w_pool": g["w_pool"][0], "pscale": f(g["pool_scale"][0].reshape(4, 128).T),
        "mhg": f(g["mh_norm_g"][0].reshape(4, 128).T), "w_out": g["w_out"][0],
        "ln1g": g["ln1_g"][0][None, :], "ln1b": g["ln1_b"][0][None, :], "ln2g": g["ln2_g"][0][None, :], "ln2b": g["ln2_b"][0][None, :],
        "w_q": f(g["w_q"][0].reshape(8, 128, 16, 128).transpose(2, 1, 0, 3)), "keysT": f(g["sub_keys"][0].reshape(16, 128, 128).transpose(2, 0, 1)),
        "u_tab": g["u_tab"][0], "v_tab": g["v_tab"][0],
    }
    shared.update(_consts())
    in_maps = []
    for b in range(8):
        sl = slice(16 * b, 16 * b + 16)
        m = dict(shared)
        m["xp"] = g["x_prompt"][b]
        m["xs"] = f(np.concatenate([g["x_sample"][sl].reshape(64, 1024), np.zeros((64, 1024), np.float32)], 0))
        m["cc"] = f(np.concatenate([g["c_prompt"][b:b + 1], g["c_sample"][sl]], 0))
        sp = g["state_pool"][0, sl]
        m["spool"] = f(sp.reshape(16, 15, 4, 128).transpose(3, 2, 0, 1))
        m["spool_raw"] = f(sp)
        m["S0"] = f(g["state_C"][0, sl])
        m["n0fm"] = f(g["state_n"][0, sl].transpose(2, 1, 0))
        sm = g["state_m"][0, sl]
        m0T = np.zeros((4, 32), np.float32); m0T[:, :16] = sm.T
        m0c = np.zeros((128, 4), np.float32); m0c[:64] = np.repeat(sm, 4, axis=0)
        m["m0T"] = m0T; m["m0col"] = m0c
        in_maps.append(m)
    nc = build_program()
    res = run_bass_kernel_spmd(nc, in_maps, core_ids=list(range(8)))
    R = res.results
    cat = lambda k: np.stack([np.asarray(r[k], dtype=np.float32) for r in R])
    y_p = cat("yp")
    y_s = cat("ys").reshape(128, 4, 1024)
    pool_p = cat("pool_p")[None]
    C_p = cat("C_p")[None]
    n_p = cat("n_p")[None]
    m_p = cat("m_p").reshape(8, 4)[None]
    pool_s = cat("pool_s").reshape(128, 15, 512)[None]
    C_s = cat("C_s").reshape(128, 4, 128, 128)[None]
    n_s = cat("n_s").reshape(128, 4, 128)[None]
    m_s = np.concatenate([np.asarray(r["m_s"], dtype=np.float32).T for r in R], 0)[None]
    return (y_p, y_s, pool_p, C_p, n_p, m_p, pool_s, C_s, n_s, m_s)
```
